# Optimizing a Trainium2 kernel written in Bass

```python
import math
import jax, jax.numpy as jnp
from jax import lax
import numpy as np

D_MODEL = 1024
BATCH = 32
SEQ = 256
DEPTH = 4
DEC_BATCH = 2
DEC_SEQ = 1024
PAST_LEN = 256

GRID_W = 64
N_MIXERS = 4
N_MOD = 9
EPS = 1e-6
FFN_HIDDEN = 2816

M_INNER = 2 * D_MODEL
M_HEADDIM = 64
M_HEADS = M_INNER // M_HEADDIM
M_GROUPS = 4
M_STATE = 128
M_CONV = 3
M_CHUNK = 128
M_CONV_CH = M_INNER + 2 * M_GROUPS * M_STATE
M_IN = M_INNER + M_CONV_CH + 2 * M_HEADS

G_CHUNK = 128
G_INNER = 2 * D_MODEL
G_HEADS = 8

F_GROUPS = 4

A_HEADS = 16
A_KV = 4
A_HD = 64
A_BLOCK = 128
ROPE_THETA = 10000.0

F32 = jnp.float32

kernel_name = 'hybrid_diffusion_macaron_ssd_gmlp_fnet_gqa_step'


def _layers_of(kind):
    return max(0, (DEPTH - kind + N_MIXERS - 1) // N_MIXERS)


def rms_norm(x, g):
    xf = x.astype(F32)
    y = xf * lax.rsqrt(jnp.mean(xf * xf, axis=-1, keepdims=True) + EPS)
    return (y * g.astype(F32)).astype(x.dtype)


def modulate(x, shift, scale):
    return x * (1 + scale[:, None]) + shift[:, None]


def adaln(cond, w, b):
    m = jax.nn.silu(cond.astype(F32)).astype(w.dtype) @ w + b
    return m.reshape(cond.shape[0], N_MOD, D_MODEL)


def swiglu(h, w_in, w_out):
    gate, up = jnp.split(h @ w_in, 2, axis=-1)
    return (jax.nn.silu(gate) * up) @ w_out


def macaron_half(x, g_norm, shift, scale, gate, w_in, w_out):
    h = modulate(rms_norm(x, g_norm), shift, scale)
    return x + 0.5 * gate[:, None] * swiglu(h, w_in, w_out)


def dw_conv_centred(x, w, b):
    k = w.shape[0]
    y = lax.conv_general_dilated(x, w[:, None, :].astype(x.dtype), window_strides=(1,),
                                 padding=[(k // 2, k // 2)],
                                 dimension_numbers=('NWC', 'WIO', 'NWC'),
                                 feature_group_count=x.shape[-1])
    return y + b


def ssd_scan(x, dt, a, bm, cm, h0):
    b, l, nh, hp = x.shape
    g, n = bm.shape[2], bm.shape[3]
    r = nh // g
    q = M_CHUNK
    nc = l // q
    xc = x.reshape(b, nc, q, g, r, hp)
    dtc = dt.reshape(b, nc, q, g, r)
    bc = bm.reshape(b, nc, q, g, n)
    cc = cm.reshape(b, nc, q, g, n)
    acum = jnp.cumsum(dtc * a.reshape(g, r), axis=2)
    seg = acum[:, :, :, None] - acum[:, :, None, :]
    lower = jnp.tril(jnp.ones((q, q), bool))[:, :, None, None]
    decay = jnp.exp(jnp.where(lower, seg, -jnp.inf))
    xdt = xc * dtc[..., None]
    cb = jnp.einsum('bcign,bcjgn->bcijg', cc, bc)
    y_diag = jnp.einsum('bcijg,bcijgr,bcjgrp->bcigrp', cb, decay, xdt)
    to_end = jnp.exp(acum[:, :, -1:] - acum)
    chunk_states = jnp.einsum('bcjgn,bcjgr,bcjgrp->bcgrpn', bc, to_end, xdt)
    chunk_decay = jnp.exp(acum[:, :, -1])

    def step(h_prev, inp):
        s, dcy = inp
        return h_prev * dcy[..., None, None] + s, h_prev

    h_final, h_in = lax.scan(step, h0.reshape(b, g, r, hp, n),
                             (jnp.moveaxis(chunk_states, 1, 0), jnp.moveaxis(chunk_decay, 1, 0)))
    h_in = jnp.moveaxis(h_in, 0, 1)
    y_off = jnp.einsum('bcign,bcgrpn,bcigr->bcigrp', cc, h_in, jnp.exp(acum))
    y = (y_diag + y_off).reshape(b, l, nh, hp)
    return y, h_final.reshape(b, nh, hp, n)


def mamba_mixer(h, h0, w_in, conv_w, conv_b, dt_bias, a_log, d_skip, norm_g, w_out):
    b, l, _ = h.shape
    proj = h @ w_in
    z = proj[..., :M_INNER]
    xbc = proj[..., M_INNER:M_INNER + M_CONV_CH]
    dt_raw = proj[..., M_INNER + M_CONV_CH:]
    xbc = jax.nn.silu(dw_conv_centred(xbc, conv_w, conv_b))
    gn = M_GROUPS * M_STATE
    xs = xbc[..., :M_INNER].reshape(b, l, M_HEADS, M_HEADDIM).astype(F32)
    bm = xbc[..., M_INNER:M_INNER + gn].reshape(b, l, M_GROUPS, M_STATE).astype(F32)
    cm = xbc[..., M_INNER + gn:].reshape(b, l, M_GROUPS, M_STATE).astype(F32)
    dt = jax.nn.softplus(dt_raw.reshape(b, l, 2, M_HEADS).astype(F32) + dt_bias.astype(F32))
    a = -jnp.exp(a_log.astype(F32))
    h0 = h0.astype(F32)
    rev = lambda t: jnp.flip(t, axis=1)
    y_f, s_f = ssd_scan(xs, dt[:, :, 0], a[0], bm, cm, h0[:, 0])
    y_b, s_b = ssd_scan(rev(xs), rev(dt[:, :, 1]), a[1], rev(bm), rev(cm), h0[:, 1])
    y = y_f + rev(y_b) + xs * d_skip.astype(F32)[:, None]
    y = y.reshape(b, l, M_INNER) * jax.nn.silu(z.astype(F32))
    yg = y.reshape(b, l, M_GROUPS, M_INNER // M_GROUPS)
    yg = yg * lax.rsqrt(jnp.mean(yg * yg, axis=-1, keepdims=True) + EPS)
    y = (yg.reshape(b, l, M_INNER) * norm_g.astype(F32)).astype(h.dtype)
    return y @ w_out, jnp.stack([s_f, s_b], axis=1).astype(h.dtype)


def gmlp_mixer(h, w_in, b_in, norm_g, w_s, b_s, w_out):
    b, l, _ = h.shape
    u, v = jnp.split(jax.nn.gelu(h @ w_in + b_in), 2, axis=-1)
    v = rms_norm(v, norm_g)
    vc = v.reshape(b, l // G_CHUNK, G_CHUNK, G_HEADS, G_INNER // G_HEADS)
    sv = jnp.einsum('hij,bcjhd->bcihd', w_s, vc) + b_s.T[:, :, None]
    return (u * sv.reshape(b, l, G_INNER)) @ w_out


def fourier_mixer(h, w_out, b_out):
    b, l, d = h.shape
    hg = h.astype(F32).reshape(b, l, F_GROUPS, d // F_GROUPS).transpose(0, 2, 1, 3)
    f = jnp.fft.fft2(hg, norm='ortho').real
    f = f.transpose(0, 2, 1, 3).reshape(b, l, d).astype(h.dtype)
    return f @ w_out + b_out


def attn_qkv(h, w_qkv, q_g, k_g):
    b, l, _ = h.shape
    qkv = h @ w_qkv
    q = qkv[..., :A_HEADS * A_HD].reshape(b, l, A_HEADS, A_HD)
    k = qkv[..., A_HEADS * A_HD:(A_HEADS + A_KV) * A_HD].reshape(b, l, A_KV, A_HD)
    v = qkv[..., (A_HEADS + A_KV) * A_HD:].reshape(b, l, A_KV, A_HD)
    return rms_norm(q, q_g), rms_norm(k, k_g), v


def _rope_1d(x, pos):
    half = x.shape[-1] // 2
    inv = ROPE_THETA ** (-jnp.arange(half, dtype=F32) / half)
    ang = pos.astype(F32)[:, None] * inv[None, :]
    cos = jnp.cos(ang)[None, :, None, :]
    sin = jnp.sin(ang)[None, :, None, :]
    x1, x2 = x[..., :half], x[..., half:]
    return jnp.concatenate([x1 * cos - x2 * sin, x1 * sin + x2 * cos], axis=-1)


def axial_rope(x):
    l = x.shape[1]
    rows = l // GRID_W
    row = jnp.repeat(jnp.arange(rows), GRID_W)
    col = jnp.tile(jnp.arange(GRID_W), rows)
    xf = x.astype(F32)
    hd2 = x.shape[-1] // 2
    out = jnp.concatenate([_rope_1d(xf[..., :hd2], row), _rope_1d(xf[..., hd2:], col)], axis=-1)
    return out.astype(x.dtype)


def gqa_attend(q, k, v):
    b, lq, nh, hd = q.shape
    grp = nh // A_KV
    nb = lq // A_BLOCK
    qb = q.reshape(b, nb, A_BLOCK, A_KV, grp, hd).transpose(1, 0, 2, 3, 4, 5)
    kf = k.astype(F32)
    vf = v.astype(F32)
    scale = hd ** -0.5

    def one_block(qblk):
        s = jnp.einsum('bqkgd,bskd->bkgqs', qblk.astype(F32), kf) * scale
        p = jax.nn.softmax(s, axis=-1)
        return jnp.einsum('bkgqs,bskd->bqkgd', p, vf).astype(q.dtype)

    o = lax.map(one_block, qb)
    return o.transpose(1, 0, 2, 3, 4, 5).reshape(b, lq, nh * hd)


def setup_inputs(seed: int = 0) -> dict:
    key = jax.random.key(seed)
    keys = iter(jax.random.split(key, 48))

    def nrm(shape, scale):
        return jax.random.normal(next(keys), shape, F32) * scale

    def gain(shape):
        return 1.0 + nrm(shape, 0.02)

    na, nb, nc, nd = (_layers_of(k) for k in range(N_MIXERS))
    D = D_MODEL
    dt0 = jnp.exp(jax.random.uniform(next(keys), (na, 2, M_HEADS), F32, math.log(1e-3), math.log(1e-1)))
    a_init = jax.random.uniform(next(keys), (na, 2, M_HEADS), F32, 1.0, 16.0)
    return {
        'x_prompt': nrm((BATCH, SEQ, D), 1.0),
        'x_sample': nrm((DEC_BATCH, DEC_SEQ, D), 1.0),
        'state_ssm': nrm((DEC_BATCH, na, 2, M_HEADS, M_HEADDIM, M_STATE), 0.5),
        'cache_k': nrm((DEC_BATCH, nd, PAST_LEN, A_KV, A_HD), 1.0),
        'cache_v': nrm((DEC_BATCH, nd, PAST_LEN, A_KV, A_HD), 1.0),
        'c': nrm((DEC_BATCH, D), 1.0),
        'c_ctx': nrm((D,), 1.0),
        'ln_g': gain((DEPTH, 3, D)),
        'ada_w': nrm((DEPTH, D, N_MOD * D), 0.5 * D ** -0.5),
        'ada_b': nrm((DEPTH, N_MOD * D), 0.02),
        'ff1_w_in': nrm((DEPTH, D, 2 * FFN_HIDDEN), D ** -0.5),
        'ff1_w_out': nrm((DEPTH, FFN_HIDDEN, D), FFN_HIDDEN ** -0.5),
        'ff2_w_in': nrm((DEPTH, D, 2 * FFN_HIDDEN), D ** -0.5),
        'ff2_w_out': nrm((DEPTH, FFN_HIDDEN, D), FFN_HIDDEN ** -0.5),
        'm_w_in': nrm((na, D, M_IN), D ** -0.5),
        'm_conv_w': nrm((na, M_CONV, M_CONV_CH), M_CONV ** -0.5),
        'm_conv_b': nrm((na, M_CONV_CH), 0.02),
        'm_dt_bias': dt0 + jnp.log(-jnp.expm1(-dt0)),
        'm_a_log': jnp.log(a_init),
        'm_d': gain((na, M_HEADS)),
        'm_norm_g': gain((na, M_INNER)),
        'm_w_out': nrm((na, M_INNER, D), M_INNER ** -0.5),
        'g_w_in': nrm((nb, D, 2 * G_INNER), D ** -0.5),
        'g_b_in': nrm((nb, 2 * G_INNER), 0.02),
        'g_norm_g': gain((nb, G_INNER)),
        'g_w_s': nrm((nb, G_HEADS, G_CHUNK, G_CHUNK), G_CHUNK ** -0.5),
        'g_b_s': gain((nb, G_HEADS, G_CHUNK)),
        'g_w_out': nrm((nb, G_INNER, D), G_INNER ** -0.5),
        'f_w_out': nrm((nc, D, D), D ** -0.5),
        'f_b_out': nrm((nc, D), 0.02),
        'a_w_qkv': nrm((nd, D, (A_HEADS + 2 * A_KV) * A_HD), D ** -0.5),
        'a_q_norm': gain((nd, A_HD)),
        'a_k_norm': gain((nd, A_HD)),
        'a_w_o': nrm((nd, A_HEADS * A_HD, D), (A_HEADS * A_HD) ** -0.5),
    }


def reference(x_prompt, x_sample, state_ssm, cache_k, cache_v, c, c_ctx,
              ln_g, ada_w, ada_b, ff1_w_in, ff1_w_out, ff2_w_in, ff2_w_out,
              m_w_in, m_conv_w, m_conv_b, m_dt_bias, m_a_log, m_d, m_norm_g, m_w_out,
              g_w_in, g_b_in, g_norm_g, g_w_s, g_b_s, g_w_out,
              f_w_out, f_b_out,
              a_w_qkv, a_q_norm, a_k_norm, a_w_o):
    xp, xs = x_prompt, x_sample
    new_ssm, new_k, new_v = [], [], []
    for i in range(DEPTH):
        kind, j = i % N_MIXERS, i // N_MIXERS
        mc = adaln(c_ctx[None], ada_w[i], ada_b[i])
        msm = adaln(c, ada_w[i], ada_b[i])
        xp = macaron_half(xp, ln_g[i, 0], mc[:, 0], mc[:, 1], mc[:, 2], ff1_w_in[i], ff1_w_out[i])
        xs = macaron_half(xs, ln_g[i, 0], msm[:, 0], msm[:, 1], msm[:, 2], ff1_w_in[i], ff1_w_out[i])
        hp = modulate(rms_norm(xp, ln_g[i, 1]), mc[:, 3], mc[:, 4])
        hs = modulate(rms_norm(xs, ln_g[i, 1]), msm[:, 3], msm[:, 4])
        if kind == 0:
            zero_state = jnp.zeros((xp.shape[0], 2, M_HEADS, M_HEADDIM, M_STATE), xp.dtype)
            mix_p, st = mamba_mixer(hp, zero_state, m_w_in[j], m_conv_w[j], m_conv_b[j], m_dt_bias[j],
                                    m_a_log[j], m_d[j], m_norm_g[j], m_w_out[j])
            mix_s, _ = mamba_mixer(hs, state_ssm[:, j], m_w_in[j], m_conv_w[j], m_conv_b[j], m_dt_bias[j],
                                   m_a_log[j], m_d[j], m_norm_g[j], m_w_out[j])
            new_ssm.append(st)
        elif kind == 1:
            mix_p = gmlp_mixer(hp, g_w_in[j], g_b_in[j], g_norm_g[j], g_w_s[j], g_b_s[j], g_w_out[j])
            mix_s = gmlp_mixer(hs, g_w_in[j], g_b_in[j], g_norm_g[j], g_w_s[j], g_b_s[j], g_w_out[j])
        elif kind == 2:
            mix_p = fourier_mixer(hp, f_w_out[j], f_b_out[j])
            mix_s = fourier_mixer(hs, f_w_out[j], f_b_out[j])
        else:
            qc, kc, vc = attn_qkv(hp, a_w_qkv[j], a_q_norm[j], a_k_norm[j])
            mix_p = gqa_attend(qc, kc, vc) @ a_w_o[j]
            new_k.append(kc)
            new_v.append(vc)
            ql, kl, vl = attn_qkv(hs, a_w_qkv[j], a_q_norm[j], a_k_norm[j])
            ql, kl = axial_rope(ql), axial_rope(kl)
            k_all = jnp.concatenate([cache_k[:, j].astype(kl.dtype), kl], axis=1)
            v_all = jnp.concatenate([cache_v[:, j].astype(vl.dtype), vl], axis=1)
            mix_s = gqa_attend(ql, k_all, v_all) @ a_w_o[j]
        xp = xp + mc[:, 5][:, None] * mix_p
        xs = xs + msm[:, 5][:, None] * mix_s
        xp = macaron_half(xp, ln_g[i, 2], mc[:, 6], mc[:, 7], mc[:, 8], ff2_w_in[i], ff2_w_out[i])
        xs = macaron_half(xs, ln_g[i, 2], msm[:, 6], msm[:, 7], msm[:, 8], ff2_w_in[i], ff2_w_out[i])
    return (xp, xs, jnp.stack(new_ssm, axis=1), jnp.stack(new_k, axis=1), jnp.stack(new_v, axis=1))
```

```python
import bisect
from contextlib import ExitStack

import numpy as np
import ml_dtypes
import concourse.bass as bass
import concourse.mybir as mybir
from concourse.bass_utils import run_bass_kernel_spmd

F32 = mybir.dt.float32
BF16 = mybir.dt.bfloat16
AF = mybir.ActivationFunctionType
ALU = mybir.AluOpType
AX = mybir.AxisListType
_ISZ = {F32: 4, BF16: 2}
NPBF = ml_dtypes.bfloat16


def _region(ap):
    sp = str(ap.space).upper()
    if 'DRAM' in sp or 'HBM' in sp:
        return None
    t = ap.tensor
    pp = 1
    for s in list(t.shape)[1:]:
        pp *= int(s)
    isz = _ISZ.get(ap.dtype, 4)
    off = int(ap.offset)
    dims = list(ap.ap)
    p0 = off // pp
    f0 = off % pp
    pstep, pcnt = int(dims[0][0]), int(dims[0][1])
    if pstep == 0:
        pcnt = 1
    ext = 0
    for st, cn in dims[1:]:
        ext += abs(int(st)) * (int(cn) - 1)
    if 'PSUM' in sp:
        return (t.name, 0, 128, 0, pp * isz)
    return (t.name, p0, p0 + pcnt, f0 * isz, (f0 + ext + 1) * isz)


class Op:
    __slots__ = ('eng', 'fn', 'deps', 'idx', 'signal', 'sigcount', 'dma_key', 'dma_cnt', 'is_dma', 'waits')

    def __init__(self, eng, fn):
        self.eng = eng
        self.fn = fn
        self.deps = set()
        self.signal = False
        self.sigcount = 0
        self.is_dma = False
        self.dma_key = None
        self.dma_cnt = 0
        self.waits = []


class Prog:
    ENGS = ('pe', 'act', 'dve', 'pool', 'sp')

    def __init__(self, nc):
        self.nc = nc
        self.ops = []
        self.acc = {}
        self.dma_state = {}
        self.dma_keys = []

    def _touch(self, op, aps_r, aps_w):
        idx = op.idx
        ops = self.ops
        for is_w, aps in ((False, aps_r), (True, aps_w)):
            for ap in aps:
                if ap is None or isinstance(ap, (int, float)):
                    continue
                r = _region(ap)
                if r is None:
                    continue
                name, p0, p1, lo, hi = r
                is_psum = 'PSUM' in str(ap.space).upper()
                lst = self.acc.get(name, [])
                keep = []
                for rec in lst:
                    q0, q1, l2, h2, w2, oi = rec
                    if oi == idx:
                        keep.append(rec)
                        continue
                    overlap = (q0 < p1 and p0 < q1 and l2 < hi and lo < h2)
                    if overlap and (is_w or w2):
                        op.deps.add(oi)
                    elif overlap and is_psum and ops[oi].eng != op.eng:
                        op.deps.add(oi)
                    contained = (q0 >= p0 and q1 <= p1 and l2 >= lo and h2 <= hi)
                    if contained:
                        o2 = ops[oi]
                        same_stream = (o2.is_dma and op.is_dma and o2.dma_key == op.dma_key) or \
                                      ((not o2.is_dma) and (not op.is_dma) and o2.eng == op.eng)
                        if is_w or (same_stream and w2 == is_w):
                            continue
                    keep.append(rec)
                keep.append((p0, p1, lo, hi, is_w, idx))
                self.acc[name] = keep

    def add(self, eng, fn, reads=(), writes=()):
        op = Op(eng, fn)
        op.idx = len(self.ops)
        self.ops.append(op)
        self._touch(op, reads, writes)
        return op

    def dma(self, out, in_, key, queue='sp', join=False):
        key = queue + '_' + str(key)
        op = Op(queue, None)
        op.idx = len(self.ops)
        op.is_dma = True
        op.dma_key = key
        st = self.dma_state.get(key)
        if st is None:
            st = [0, None]
            self.dma_state[key] = st
            self.dma_keys.append(key)
        if st[1] is not None and not join:
            op.deps.add(st[1])
        st[0] += 16
        op.dma_cnt = st[0]
        st[1] = op.idx
        op.fn = lambda e, o=out, i=in_: e.dma_start(out=o, in_=i)
        self.ops.append(op)
        self._touch(op, [in_], [out])
        return op

    def mm(self, out, lhsT, rhs, start=True, stop=True):
        return self.add('pe', lambda e: e.matmul(out, lhsT, rhs, start=start, stop=stop),
                        [lhsT, rhs] + ([] if start else [out]), [out])

    def transpose(self, out, in_, ident):
        return self.add('pe', lambda e: e.transpose(out, in_, ident), [in_, ident], [out])

    def act(self, out, in_, func, scale=1.0, bias=None, accum_out=None):
        kw = {}
        if bias is not None:
            kw['bias'] = bias
        if accum_out is not None:
            kw['accum_out'] = accum_out
        rd = [in_, scale, bias]
        return self.add('act', lambda e: e.activation(out=out, in_=in_, func=func, scale=scale, **kw),
                        rd, [out, accum_out])

    def tt(self, out, in0, in1, op, eng='dve'):
        return self.add(eng, lambda e: e.tensor_tensor(out=out, in0=in0, in1=in1, op=op), [in0, in1], [out])

    def ts(self, out, in0, s1, op0, s2=None, op1=None, eng='dve'):
        kw = {}
        if op1 is not None:
            kw['op1'] = op1
        return self.add(eng, lambda e: e.tensor_scalar(out=out, in0=in0, scalar1=s1, scalar2=s2, op0=op0, **kw),
                        [in0, s1, s2], [out])

    def stt(self, out, in0, scalar, in1, op0, op1):
        return self.add('dve', lambda e: e.scalar_tensor_tensor(out=out, in0=in0, scalar=scalar, in1=in1,
                                                                op0=op0, op1=op1), [in0, in1, scalar], [out])

    def copy(self, out, in_, eng='dve'):
        if eng == 'act':
            return self.add('act', lambda e: e.copy(out=out, in_=in_), [in_], [out])
        return self.add(eng, lambda e: e.tensor_copy(out=out, in_=in_), [in_], [out])

    def memset(self, ap, val, eng='dve'):
        return self.add(eng, lambda e: e.memset(ap, val), [], [ap])

    def reduce(self, out, in_, op, axis=AX.X):
        return self.add('dve', lambda e: e.tensor_reduce(out=out, in_=in_, op=op, axis=axis), [in_], [out])

    def recip(self, out, in_):
        return self.add('dve', lambda e: e.reciprocal(out=out, in_=in_), [in_], [out])

    def emit(self, stack):
        nc = self.nc
        ops = self.ops
        for op in ops:
            for d in op.deps:
                o2 = ops[d]
                if not o2.is_dma:
                    if o2.eng == 'pe' and op.eng == 'pe' and not op.is_dma:
                        continue
                    o2.signal = True
        cnt = {e: 0 for e in self.ENGS}
        for op in ops:
            if op.is_dma:
                continue
            if op.signal:
                cnt[op.eng] += 1
            op.sigcount = cnt[op.eng]
        key_hist = {}
        for op in ops:
            if op.is_dma:
                key_hist.setdefault(op.dma_key, []).append((op.idx, op.dma_cnt))
        waited = {e: {} for e in self.ENGS}
        for op in ops:
            need = {}
            for d in op.deps:
                o2 = ops[d]
                if o2.is_dma:
                    h = key_hist[o2.dma_key]
                    pos = bisect.bisect_left(h, (op.idx, 0)) - 1
                    c = h[pos][1] if pos >= 0 else o2.dma_cnt
                    if op.is_dma and op.dma_key == o2.dma_key:
                        c = o2.dma_cnt
                    c = max(c, o2.dma_cnt)
                    k = ('dma', o2.dma_key)
                    need[k] = max(need.get(k, 0), c)
                else:
                    if o2.eng == 'pe' and op.eng == 'pe' and not op.is_dma:
                        continue
                    k = ('eng', o2.eng)
                    need[k] = max(need.get(k, 0), o2.sigcount)
            w = waited[op.eng]
            for k, v in need.items():
                if w.get(k, 0) >= v:
                    continue
                w[k] = v
                op.waits.append((k, v))
        final_waits = [(('dma', k), self.dma_state[k][0]) for k in self.dma_keys]
        sems = {}
        for e in self.ENGS:
            sems[('eng', e)] = stack.enter_context(nc.semaphore('s_' + e))
        for k in self.dma_keys:
            sems[('dma', k)] = stack.enter_context(nc.semaphore('d_' + str(k)))
        self.n_sems = len(sems)
        self.sig_counts = cnt
        block = stack.enter_context(nc.Block())
        byeng = {e: [op for op in ops if op.eng == e] for e in self.ENGS}

        def run(e, engname):
            for op in byeng[engname]:
                for k, v in op.waits:
                    e.wait_ge(sems[k], v)
                ins = op.fn(e)
                if op.is_dma:
                    ins.then_inc(sems[('dma', op.dma_key)], 16)
                elif op.signal:
                    ins.then_inc(sems[('eng', engname)], 1)
            if engname == 'sp':
                for k, v in final_waits:
                    e.wait_ge(sems[k], v)

        @block.tensor
        def _(e):
            run(e, 'pe')

        @block.scalar
        def _(e):
            run(e, 'act')

        @block.vector
        def _(e):
            run(e, 'dve')

        @block.gpsimd
        def _(e):
            run(e, 'pool')

        @block.sync
        def _(e):
            run(e, 'sp')


D = 1024
NT = 1280
FH = 2816
NSLOT = 3
SLOT = 4096
ARB = 30720
ARF = 5120
TG = [(0, 512), (512, 512), (1024, 256)]
EPS = 1e-6
NEG = -30000.0


def cidx(t0):
    return 0 if t0 < 1024 else 1


def bc(ap, shape):
    return ap.to_broadcast(list(shape))


class K:
    pass


def run_pipelined(gens, skew=1):
    active = []
    gens = list(gens)
    i = 0
    while i < len(gens) or active:
        if i < len(gens):
            active.append([gens[i], 0])
            i += 1
        for a in list(active):
            if a[1] % skew == 0:
                try:
                    next(a[0])
                except StopIteration:
                    active.remove(a)
                    continue
            a[1] += 1


def build(nlayers=4, mixers=(True, True, True, True), ffn=True, mstop=99, lite=False):
    nc = bass.Bass("TRN2", target_bir_lowering=False)
    st = ExitStack()
    S = K()

    S.declared = set()
    BIG = {"ada_w": None, "ff1_w_in": None, "ff2_w_in": None, "ff1_w_out": None, "ff2_w_out": None,
           "m_w_in": 0, "m_w_out": 0, "g_w_in": 1, "g_w_out": 1, "f_w_out": 2, "a_w_qkv": 3, "a_w_o": 3}

    def din(name, shape, dt=F32):
        if lite and name in BIG:
            kind = BIG[name]
            if kind is None or not (mixers[kind] and nlayers > kind):
                return None
        S.declared.add(name)
        return nc.dram_tensor(name, list(shape), dt, kind="ExternalInput").ap()

    def dout(name, shape, dt=F32):
        return nc.dram_tensor(name, list(shape), dt, kind="ExternalOutput").ap()

    def sb(name, shape, dt=F32):
        return st.enter_context(nc.sbuf_tensor(name, list(shape), dt))

    xin = din("xin", [NT, D])
    condT = din("condT", [128, 8, 2])
    flag_d = din("flag", [128, 1])
    ssm0 = din("ssm0", [2, 128, 2048])
    ck_d = din("ck", [256, 256])
    cv_d = din("cv", [256, 256])
    amask_d = din("amask", [10, 128, 1024], BF16)
    ropec_d = din("ropec", [128, 10, 32])
    ropes_d = din("ropes", [128, 10, 32])
    dftpA_d = din("dftpA", [128, 8, 2, 1024], BF16)
    dftpB_d = din("dftpB", [128, 2, 2, 256], BF16)
    dftc_d = din("dftc", [128, 2, 512], BF16)
    cst_f = din("cst_f", [128, 4, 128])
    cst_b = din("cst_b", [128, 7, 128], BF16)
    ln_gT = din("ln_gT", [128, 4, 3, 8])
    ada_bT = din("ada_bT", [128, 4, 72])
    ada_w = din("ada_w", [4, D, 9 * D])
    ff_w_in = [din("ff1_w_in", [4, D, 2 * FH]), din("ff2_w_in", [4, D, 2 * FH])]
    ff_w_out = [din("ff1_w_out", [4, FH, D]), din("ff2_w_out", [4, FH, D])]
    m_w_in = din("m_w_in", [1, D, 5184])
    m_convw = din("m_convw", [128, 24, 3])
    m_convb = din("m_convb", [128, 24])
    m_dtb = din("m_dtb", [128, 64])
    m_alog = din("m_alog", [128, 64])
    m_dd = din("m_dd", [128, 32])
    m_ng = din("m_ng", [128, 2048])
    m_w_out = din("m_w_out", [1, 2048, D])
    g_w_in = din("g_w_in", [1, D, 4096])
    g_buT = din("g_buT", [128, 16])
    g_bv = din("g_bv", [1, 2048])
    g_ng = din("g_ng", [128, 2048])
    g_w_s = din("g_w_s", [1, 8, 128, 128])
    g_b_s = din("g_b_s", [1, 1024])
    g_w_out = din("g_w_out", [1, 2048, D])
    f_w_out = din("f_w_out", [1, D, D])
    f_bT = din("f_bT", [128, 8])
    a_w_qkv = din("a_w_qkv", [1, D, 1536])
    a_qg = din("a_qg", [128, 64])
    a_kg = din("a_kg", [128, 64])
    a_w_o = din("a_w_o", [1, D, D])

    yout = dout("yout", [NT, D])
    ssm_out = dout("ssm_out", [5, 2, 128, 2048])
    newk = dout("newk", [NT, 256])
    newv = dout("newv", [NT, 256])

    xT = sb("xT", [128, 8, NT])
    hT = sb("hT", [128, 8, 1300], BF16)
    arb = sb("arb", [128, ARB], BF16)
    arf = sb("arf", [128, ARF])
    slots = [sb("slot%d" % i, [128, SLOT], BF16) for i in range(NSLOT)]
    cf = sb("cf", [128, 4, 128])
    cb = sb("cb", [128, 7, 128], BF16)
    sqb = sb("sqb", [128, 8, 512], BF16)
    tmps = [sb("tmp%d" % i, [128, 512]) for i in range(4)]
    io = [sb("io%d" % i, [128, 512]) for i in range(2)]
    rs = sb("rs", [128, 512])
    modT = sb("modT", [128, 72, 2])
    scT = sb("scT", [128, 3, 8, 2])
    gtT = sb("gtT", [128, 3, 8, 2])
    lng = sb("lng", [128, 4, 3, 8])
    adab = sb("adab", [128, 4, 72])
    cnd = sb("cnd", [128, 8, 2])
    siluc = sb("siluc", [128, 8, 2], BF16)
    small = sb("small", [128, 64])
    misc = sb("misc", [128, 1856])
    miscb = sb("miscb", [128, 3072], BF16)
    psb = [st.enter_context(nc.psum_tensor("ps%d" % i, [128, 512], F32)) for i in range(7)]
    ptb = st.enter_context(nc.psum_tensor("ptb", [128, 1024], BF16))

    P = Prog(nc)
    S.marks = []

    def mark(label):
        S.marks.append((label, sum(1 for o in P.ops if o.eng == 'pe')))
    S.ti = 0
    S.si = 0
    S.bi = 0
    S.rr = 0
    S.em = 0

    S.tring = tmps

    def tmp():
        S.ti += 1
        return S.tring[S.ti % len(S.tring)]

    S.nbank = 7

    def bank(lo=0, hi=None):
        hi = S.nbank if hi is None else hi
        S.bi += 1
        return psb[lo + (S.bi % (hi - lo))]

    identf = cf[:, 0, :]
    trif = cf[:, 1, :]
    trib = cf[:, 2, :]
    onesf = cf[:, 3, :]
    identb = cb[:, 0, :]
    trifb = cb[:, 1, :]
    tribb = cb[:, 2, :]
    onesb = cb[:, 3, :]
    negonesb = cb[:, 4, :]
    nmfb = cb[:, 5, :]
    nmbb = cb[:, 6, :]
    eps_ap = small[:, 0:1]
    one_ap = small[:, 1:2]
    flag_ap = small[:, 2:3]

    def carve(t, off, shape):
        n = 1
        for s in shape:
            n *= s
        v = t[:, off:off + n]
        if len(shape) == 1:
            return v
        names = " ".join("d%d" % i for i in range(len(shape)))
        kw = {"d%d" % i: shape[i] for i in range(1, len(shape))}
        return v.rearrange("p (%s) -> p %s" % (names, names), **kw)

    def wslot():
        s = slots[S.si % NSLOT]
        key = "w%d" % (S.si % NSLOT)
        S.si += 1
        return s, key

    def wload(src, kt, ncols, parts=128):
        s, key = wslot()
        v = s[0:parts, 0:kt * ncols].rearrange("p (k n) -> p k n", n=ncols)
        P.dma(v, src, key=key, queue='pool')
        return v

    def wsrc(w2d, c0, ncols, r0=0, rows=None, p=128):
        rows = rows if rows is not None else w2d.shape[0]
        return w2d[r0:r0 + rows, c0:c0 + ncols].rearrange("(k p) n -> p k n", p=p)

    P.dma(cf[:], cst_f, key="c0")
    P.dma(cb[:], cst_b, key="c1")
    P.dma(lng[:], ln_gT, key="c2")
    P.dma(adab[:], ada_bT, key="c3")
    P.dma(cnd[:], condT, key="c4")
    P.dma(small[:, 2:3], flag_d, key="c5")
    P.memset(small[:, 0:1], EPS)
    P.memset(small[:, 1:2], 1.0)
    P.act(siluc[:], cnd[:], AF.Silu)

    def xload_gen():
        for c in range(10):
            for kq in range(2):
                iob = io[kq]
                P.dma(iob[:], xin[c * 128:(c + 1) * 128, kq * 512:(kq + 1) * 512], key="io%d" % kq)
                pb = bank(0, 6)
                for kk in range(4):
                    P.transpose(pb[:, kk * 128:(kk + 1) * 128], iob[:, kk * 128:(kk + 1) * 128], identf)
                P.copy(xT[:, kq * 4:(kq + 1) * 4, c * 128:(c + 1) * 128],
                       pb[:, :].rearrange("p (a b) -> p a b", a=4), eng=('act' if kq else 'dve'))
            yield

    def adaln_gen(li):
        pm = psb[6]
        aw = ada_w[li]
        for blk in range(18):
            wv = wload(wsrc(aw, blk * 512, 512), 8, 512)
            for jj in range(4):
                j = blk * 4 + jj
                for k in range(8):
                    P.mm(pm[:, 2 * j:2 * j + 2], wv[:, k, jj * 128:(jj + 1) * 128], siluc[:, k, :],
                         start=(k == 0), stop=(k == 7))
            yield

    def adaln_finish(li):
        pm = psb[6]
        P.tt(modT[:], pm[:, 0:144].rearrange("p (j c) -> p j c", c=2),
             bc(adab[:, li, :].unsqueeze(2), [128, 72, 2]), ALU.add)
        for n in range(3):
            t = tmp()
            tv = t[:, 0:16].rearrange("p (k c) -> p k c", c=2)
            P.ts(tv, modT[:, (3 * n + 1) * 8:(3 * n + 2) * 8, :], 1.0, ALU.add)
            P.tt(scT[:, n], tv, bc(lng[:, li, n, :].unsqueeze(2), [128, 8, 2]), ALU.mult)
            P.ts(gtT[:, n], modT[:, (3 * n + 2) * 8:(3 * n + 3) * 8, :], (1.0 if n == 1 else 0.5), ALU.mult)

    def sh_ap(n, k, ci):
        return modT[:, 3 * n * 8 + k, ci:ci + 1]

    S.ss = None

    def rmsnorm_mod(n, groups, dst):
        ss = S.ss
        S.ss = None
        for (t0, N) in groups:
            ci = cidx(t0)
            if ss is None:
                for k in range(8):
                    P.act(sqb[:, k, :N], xT[:, k, t0:t0 + N], AF.Square)
                pb = bank()
                for k in range(8):
                    P.mm(pb[:, :N], onesb, sqb[:, k, :N], start=(k == 0), stop=(k == 7))
                ssv = pb[:, :N]
            else:
                c0 = t0 % 512
                ssv = ss[t0 // 512][:, c0:c0 + N]
            P.act(rs[:, :N], ssv, AF.Sqrt, scale=1.0 / D, bias=eps_ap)
            P.recip(rs[:, :N], rs[:, :N])
            for k in range(8):
                t = tmp()
                P.tt(t[:, :N], xT[:, k, t0:t0 + N], rs[:, :N], ALU.mult)
                P.act(dst(k, t0, N), t[:, :N], AF.Identity, scale=scT[:, n, k, ci:ci + 1], bias=sh_ap(n, k, ci))

    def hplain(k, t0, N):
        return hT[:, k, t0:t0 + N]

    def resid_add(m, t0, N, pb, n):
        ci = cidx(t0)
        P.stt(xT[:, m, t0:t0 + N], pb[:, :N], gtT[:, n, m, ci:ci + 1], xT[:, m, t0:t0 + N], ALU.mult, ALU.add)

    def ffn_half(li, which, n, adagen=None, ss_next=True):
        rmsnorm_mod(n, TG, hplain)
        hid = carve(arb, 0, [22, NT])
        w_in = ff_w_in[which][li]
        w_out = ff_w_out[which][li]
        for jb in range(11):
            s, key = wslot()
            wv = s[:, 0:8 * 512].rearrange("p (k n) -> p k n", n=512)
            P.dma(wv[:, :, 0:256], wsrc(w_in, jb * 256, 256), key=key, queue='pool')
            P.dma(wv[:, :, 256:512], wsrc(w_in, FH + jb * 256, 256), key=key, queue='pool', join=True)
            for (t0, N) in TG:
                for half in range(2):
                    pg = bank()
                    pu = bank()
                    for k in range(8):
                        P.mm(pg[:, :N], wv[:, k, half * 128:(half + 1) * 128], hT[:, k, t0:t0 + N],
                             start=(k == 0), stop=(k == 7))
                    for k in range(8):
                        P.mm(pu[:, :N], wv[:, k, 256 + half * 128:256 + (half + 1) * 128], hT[:, k, t0:t0 + N],
                             start=(k == 0), stop=(k == 7))
                    t = tmp()
                    P.act(t[:, :N], pg[:, :N], AF.Silu)
                    P.tt(hid[:, jb * 2 + half, t0:t0 + N], t[:, :N], pu[:, :N], ALU.mult)
            if adagen is not None:
                next(adagen, None)
        nb_save = S.nbank
        ssb = [psb[3], psb[4], psb[5]]
        if ss_next:
            S.nbank = 3
        for m in range(8):
            wv = wload(wsrc(w_out, m * 128, 128), 22, 128)
            for gi, (t0, N) in enumerate(TG):
                pb = bank()
                for kk in range(22):
                    P.mm(pb[:, :N], wv[:, kk, :], hid[:, kk, t0:t0 + N], start=(kk == 0), stop=(kk == 21))
                resid_add(m, t0, N, pb, n)
                if ss_next:
                    sq = sqb[:, (m * 3 + gi) % 8, :N]
                    P.act(sq, xT[:, m, t0:t0 + N], AF.Square)
                    P.mm(ssb[gi][:, :N], onesb, sq, start=(m == 0), stop=(m == 7))
            if adagen is not None:
                next(adagen, None)
        S.nbank = nb_save
        if ss_next:
            S.ss = ssb
        if adagen is not None:
            for _ in adagen:
                pass

    def mixer_gmlp():
        wi = g_w_in[0]
        wo = g_w_out[0]
        gng = carve(arf, 128, [2048])
        P.dma(gng, g_ng, key="mc0")
        buT = carve(arf, 0, [16])
        P.dma(buT, g_buT, key="mc1")
        ssq = carve(arf, 16, [10, 4])
        rst = carve(arf, 64, [16])
        P.memset(miscb[:, :], 0.0)
        bvrow = miscb[0:1, 0:2048]
        bsrow = miscb[0:1, 2048:3072]
        P.dma(bvrow, g_bv, key="mc2", queue='pool')
        P.dma(bsrow, g_b_s, key="mc3", queue='pool')
        bvall = miscb[:, 0:2048]
        bsall = miscb[:, 2048:3072]
        wsn = carve(arb, 0, [8, 128])
        wsT = carve(arb, 1024, [8, 128])
        P.dma(wsn, g_w_s[0].rearrange("h i j -> i h j"), key="mc4", queue='pool')
        for hq in range(2):
            for hh in range(4):
                h = hq * 4 + hh
                P.transpose(ptb[:, hh * 128:(hh + 1) * 128], wsn[:, h, :], identb)
            P.copy(wsT[:, hq * 4:(hq + 1) * 4, :], ptb[:, 0:512].rearrange("p (a b) -> p a b", a=4))
        v_tm = carve(arb, 2048, [5, 2048])
        svT = carve(arb, 2048 + 10240, [16, 640])
        vn = carve(arb, 2048 + 20480, [2048])
        halves = [[(0, 512), (512, 128)], [(640, 384), (1024, 256)]]
        for half in range(2):
            T0 = 640 * half
            chunks = list(range(5 * half, 5 * half + 5))
            for vb in range(4):
                wv = wload(wsrc(wi, 2048 + vb * 512, 512), 8, 512)
                for ci_, c in enumerate(chunks):
                    pb = bank()
                    for k in range(8):
                        P.mm(pb[:, :], hT[:, k, c * 128:(c + 1) * 128], wv[:, k, :], start=(k == 0), stop=False)
                    P.mm(pb[:, :], onesb, bvall[:, vb * 512:(vb + 1) * 512], start=False, stop=True)
                    t = tmp()
                    P.act(t[:, :], pb[:, :], AF.Gelu_apprx_tanh)
                    P.act(sqb[:, vb, :], t[:, :], AF.Square, accum_out=ssq[:, c, vb:vb + 1])
                    P.copy(v_tm[:, ci_, vb * 512:(vb + 1) * 512], t[:, :])
            for ci_, c in enumerate(chunks):
                P.reduce(rst[:, 0:1], ssq[:, c, :], ALU.add)
                P.act(rst[:, 0:1], rst[:, 0:1], AF.Sqrt, scale=1.0 / 2048, bias=eps_ap)
                P.recip(rst[:, 0:1], rst[:, 0:1])
                P.stt(vn, v_tm[:, ci_, :], rst[:, 0:1], gng, ALU.mult, ALU.mult)
                for dq in range(4):
                    pb = bank()
                    for dd in range(4):
                        d = dq * 4 + dd
                        hh = d // 2
                        P.mm(pb[:, dd * 128:(dd + 1) * 128], vn[:, d * 128:(d + 1) * 128], wsT[:, hh, :],
                             start=True, stop=False)
                        P.mm(pb[:, dd * 128:(dd + 1) * 128], onesb, bsall[:, hh * 128:(hh + 1) * 128],
                             start=False, stop=True)
                    P.copy(svT[:, dq * 4:(dq + 1) * 4, ci_ * 128:(ci_ + 1) * 128],
                           pb[:, :].rearrange("p (a b) -> p a b", a=4), eng='act')
            for ub in range(4):
                wv = wload(wsrc(wi, ub * 512, 512), 8, 512)
                for (t0, N) in halves[half]:
                    for dd in range(4):
                        d = ub * 4 + dd
                        pb = bank()
                        for k in range(8):
                            P.mm(pb[:, :N], wv[:, k, dd * 128:(dd + 1) * 128], hT[:, k, t0:t0 + N],
                                 start=(k == 0), stop=(k == 7))
                        t = tmp()
                        P.act(t[:, :N], pb[:, :N], AF.Gelu_apprx_tanh, bias=buT[:, d:d + 1])
                        P.tt(svT[:, d, t0 - T0:t0 - T0 + N], t[:, :N], svT[:, d, t0 - T0:t0 - T0 + N], ALU.mult)
            for mb in range(4):
                wv = wload(wsrc(wo, mb * 256, 256), 16, 256)
                for mh in range(2):
                    m = mb * 2 + mh
                    for (t0, N) in halves[half]:
                        pb = bank()
                        for kk in range(16):
                            P.mm(pb[:, :N], wv[:, kk, mh * 128:(mh + 1) * 128], svT[:, kk, t0 - T0:t0 - T0 + N],
                                 start=(kk == 0), stop=(kk == 15))
                        resid_add(m, t0, N, pb, 1)

    def mixer_fnet():
        dftc = carve(miscb, 0, [2, 512])
        P.dma(dftc, dftc_d, key="mc0")
        dA = carve(arb, 0, [8, 2, 1024])
        dB = carve(arb, 16384, [2, 2, 256])
        UV = carve(arb, 17408, [10, 512])
        P.dma(dA, dftpA_d, key="mc1")
        P.dma(dB, dftpB_d, key="mc2")
        fb = carve(arf, 0, [8])
        P.dma(fb, f_bT, key="mc3")
        for gq in range(4):
            for c in range(10):
                pb = bank()
                for k in range(2):
                    P.mm(pb[:, :], hT[:, 2 * gq + k, c * 128:(c + 1) * 128], dftc[:, k, :], start=(k == 0), stop=(k == 1))
                P.copy(UV[:, c, :], pb[:, :], eng=('act' if c % 2 else 'dve'))
            for dd in range(2):
                d = 2 * gq + dd
                for lg in range(2):
                    pb = bank()
                    for lt in range(8):
                        P.mm(pb[:, :], UV[:, lt, dd * 128:(dd + 1) * 128], dA[:, lt, 0, lg * 512:(lg + 1) * 512],
                             start=(lt == 0), stop=False)
                        P.mm(pb[:, :], UV[:, lt, 256 + dd * 128:256 + (dd + 1) * 128],
                             dA[:, lt, 1, lg * 512:(lg + 1) * 512], start=False, stop=(lt == 7))
                    P.copy(hT[:, d, lg * 512:(lg + 1) * 512], pb[:, :], eng=('act' if lg else 'dve'))
                pb = bank()
                for lt in range(2):
                    P.mm(pb[:, :256], UV[:, 8 + lt, dd * 128:(dd + 1) * 128], dB[:, lt, 0, :], start=(lt == 0), stop=False)
                    P.mm(pb[:, :256], UV[:, 8 + lt, 256 + dd * 128:256 + (dd + 1) * 128], dB[:, lt, 1, :],
                         start=False, stop=(lt == 1))
                P.copy(hT[:, d, 1024:1280], pb[:, :256], eng='act')
        wo = f_w_out[0]
        for mb in range(2):
            wv = wload(wsrc(wo, mb * 512, 512), 8, 512)
            for mm_ in range(4):
                m = mb * 4 + mm_
                for (t0, N) in TG:
                    ci = cidx(t0)
                    pb = bank()
                    for k in range(8):
                        P.mm(pb[:, :N], wv[:, k, mm_ * 128:(mm_ + 1) * 128], hT[:, k, t0:t0 + N],
                             start=(k == 0), stop=(k == 7))
                    t = tmp()
                    P.ts(t[:, :N], pb[:, :N], fb[:, m:m + 1], ALU.add, gtT[:, 1, m, ci:ci + 1], ALU.mult)
                    P.tt(xT[:, m, t0:t0 + N], xT[:, m, t0:t0 + N], t[:, :N], ALU.add)

    def mixer_attn():
        wq = a_w_qkv[0]
        wo = a_w_o[0]
        QT = carve(arb, 0, [8, NT])
        KT = carve(arb, 10240, [4, 1536])
        Vtm = carve(arb, 16384, [12, 320])
        mask = carve(arb, 20224, [10, 1024])
        ckb = carve(miscb, 0, [2, 256])
        P.memset(QT[64:128, :, :], 0.0)
        P.memset(KT[64:128, :, :], 0.0)
        P.memset(Vtm[:, :, 256:320], 0.0)
        P.dma(mask, amask_d.rearrange("t p q -> p t q"), key="mc0")
        qg = misc[:, 0:64]
        kg = misc[:, 64:128]
        rc = carve(misc, 128, [10, 32])
        rsn = carve(misc, 448, [10, 32])
        P.dma(qg, a_qg, key="mc1")
        P.dma(kg, a_kg, key="mc2")
        P.dma(rc, ropec_d, key="mc3")
        P.dma(rsn, ropes_d, key="mc4")
        P.dma(ckb, ck_d.rearrange("(t p) f -> p t f", p=128), key="mc5", queue='pool')
        P.dma(Vtm[:, 0:2, 0:256], cv_d.rearrange("(t p) f -> p t f", p=128), key="mc6", queue='pool')
        for t in range(2):
            for kh in range(4):
                P.transpose(ptb[0:64, kh * 128:(kh + 1) * 128], ckb[:, t, kh * 64:(kh + 1) * 64], identb)
            P.copy(KT[0:64, :, t * 128:(t + 1) * 128], ptb[0:64, 0:512].rearrange("p (a b) -> p a b", a=4))

        ms_ = [carve(arf, 0, [8]), carve(arf, 8, [8])]
        qr_ = [carve(miscb, 0, [512]), carve(miscb, 512, [512])]
        pt_ = [carve(miscb, 1024, [512]), carve(miscb, 1536, [512]), carve(miscb, 2048, [512])]

        def proj_unit(wv, iskv, qgain, c, u):
            p = u % 2
            H = 4 if iskv else 8
            W = H * 64
            pb = bank()
            for k in range(8):
                P.mm(pb[:, :], hT[:, k, c * 128:(c + 1) * 128], wv[:, k, :], start=(k == 0), stop=(k == 7))
            yield
            pbv = pb[:, 0:W]
            t1 = tmps[2 * p]
            t2 = tmps[2 * p + 1]
            ms = ms_[p]
            qr = qr_[p]
            iob = io[c % 2]
            t3 = iob[:, 0:256] if iskv else [rs, io[0]][p][:, :W]
            P.act(t1[:, :W], pbv, AF.Square)
            P.reduce(ms[:, 0:H], t1[:, :W].rearrange("p (h d) -> p h d", d=64), ALU.add)
            P.act(ms[:, 0:H], ms[:, 0:H], AF.Sqrt, scale=1.0 / 64, bias=eps_ap)
            P.recip(ms[:, 0:H], ms[:, 0:H])
            t2v = t2[:, :W].rearrange("p (h d) -> p h d", d=64)
            P.tt(t2v, pbv.rearrange("p (h d) -> p h d", d=64), bc(ms[:, 0:H].unsqueeze(2), [128, H, 64]), ALU.mult)
            t3v = t3.rearrange("p (h d) -> p h d", d=64)
            P.tt(t3v, t2v, bc(qgain.unsqueeze(1), [128, H, 64]), ALU.mult)
            if iskv:
                P.copy(iob[:, 256:512], pb[:, 256:512], eng='act')
                P.copy(Vtm[:, 2 + c, 0:256], pb[:, 256:512], eng='act')
            yield
            x5 = t3.rearrange("p (h a b f) -> p h a b f", h=H, a=2, b=2)
            o5 = qr[:, 0:W].rearrange("p (h a b f) -> p h a b f", h=H, a=2, b=2)
            x1 = x5[:, :, :, 0, :]
            x2 = x5[:, :, :, 1, :]
            cosb = bc(rc[:, c, :].rearrange("p (a f) -> p a f", a=2).unsqueeze(1), [128, H, 2, 16])
            sinb = bc(rsn[:, c, :].rearrange("p (a f) -> p a f", a=2).unsqueeze(1), [128, H, 2, 16])
            Wh = W // 2
            tav = t1[:, :Wh].rearrange("p (h a f) -> p h a f", h=H, a=2)
            tbv = t2[:, :Wh].rearrange("p (h a f) -> p h a f", h=H, a=2)
            P.tt(tav, x1, cosb, ALU.mult)
            P.tt(tbv, x2, sinb, ALU.mult)
            P.tt(o5[:, :, :, 0, :], tav, tbv, ALU.subtract)
            P.tt(tav, x1, sinb, ALU.mult)
            P.tt(tbv, x2, cosb, ALU.mult)
            P.tt(o5[:, :, :, 1, :], tav, tbv, ALU.add)
            if iskv:
                P.dma(newk[c * 128:(c + 1) * 128, :], iob[:, 0:256], key="ok%d" % (c % 2))
                P.dma(newv[c * 128:(c + 1) * 128, :], iob[:, 256:512], key="ov%d" % (c % 2))
            yield
            for h in range(H):
                P.transpose(ptb[0:64, h * 128:(h + 1) * 128], qr[:, h * 64:(h + 1) * 64], identb)
            yield
            if iskv:
                P.copy(KT[0:64, :, 256 + c * 128:256 + (c + 1) * 128],
                       ptb[0:64, 0:512].rearrange("p (a b) -> p a b", a=4), eng='act')
            else:
                P.copy(QT[0:64, :, c * 128:(c + 1) * 128], ptb[0:64, :].rearrange("p (a b) -> p a b", a=8), eng='act')

        mark('a kv')
        wv = wload(wsrc(wq, 1024, 512), 8, 512)
        run_pipelined([proj_unit(wv, True, kg, c, c) for c in range(10)])
        for qb in range(2):
            mark('a q%d proj' % qb)
            wv = wload(wsrc(wq, qb * 512, 512), 8, 512)
            run_pipelined([proj_unit(wv, False, qg, c, c) for c in range(10)])
            mark('a q%d core' % qb)

            def att_unit(hl, kvh, q0, N, si, s, ntl, usemask, po, psm, ctr):
                ps_ = psb[ctr % 3]
                pt = pt_[ctr % 3]
                P.mm(ps_[:, :N], KT[:, kvh, s * 128:(s + 1) * 128], QT[:, hl, q0:q0 + N],
                     start=True, stop=(not usemask))
                if usemask:
                    P.mm(ps_[:, :N], identb, mask[:, s, q0:q0 + N], start=False, stop=True)
                yield
                P.act(pt[:, :N], ps_[:, :N], AF.Exp, scale=0.125)
                yield
                P.mm(po[:, :N], Vtm[:, s, kvh * 64:kvh * 64 + 128], pt[:, :N], start=(si == 0), stop=(si == ntl - 1))
                P.mm(psm[:, :N], onesb, pt[:, :N], start=(si == 0), stop=(si == ntl - 1))
                if si == ntl - 1:
                    yield
                    t = tmp()
                    P.recip(t[0:64, :N], psm[0:64, :N])
                    P.tt(QT[0:64, hl, q0:q0 + N], po[0:64, :N], t[0:64, :N], ALU.mult)

            units = []
            it = 0
            ctr = 0
            for hl in range(8):
                kvh = qb * 2 + hl // 4
                jobs = [(0, 512, list(range(10)), True), (512, 512, list(range(10)), True), (1024, 256, [10, 11], False)]
                for (q0, N, tiles, usemask) in jobs:
                    po = psb[3 + 2 * (it % 2)]
                    psm = psb[4 + 2 * (it % 2)]
                    it += 1
                    for si, s_ in enumerate(tiles):
                        units.append(att_unit(hl, kvh, q0, N, si, s_, len(tiles), usemask, po, psm, ctr))
                        ctr += 1
            run_pipelined(units)
            mark('a q%d out' % qb)
            for mb in range(2):
                src = wo[qb * 512:(qb + 1) * 512, mb * 512:(mb + 1) * 512].rearrange("(h p) n -> p h n", p=64)
                wv2_ = wload(src, 8, 512, parts=64)
                wv2 = slots[(S.si - 1) % NSLOT][:, 0:8 * 512].rearrange("p (k n) -> p k n", n=512)
                for mm_ in range(4):
                    m = mb * 4 + mm_
                    for (t0, N) in TG:
                        pb = bank(0, 3)
                        for hl in range(8):
                            P.mm(pb[:, :N], wv2[:, hl, mm_ * 128:(mm_ + 1) * 128], QT[:, hl, t0:t0 + N],
                                 start=(hl == 0), stop=(hl == 7))
                        resid_add(m, t0, N, pb, 1)

    def mixer_mamba():
        wi = m_w_in[0]
        wo = m_w_out[0]
        hp = hT[:, :, :].rearrange("p k (s w) -> p k s w", w=260)

        def hch(k, c):
            o = 2 + (c % 2) * 128
            return hp[:, k, c // 2, o:o + 128]

        P.memset(hp[:, :, :, 0:2], 0.0)
        P.memset(hp[:, :, :, 258:260], 0.0)
        P.ts(hp[:, :, 1:4, 1:2], hp[:, :, 0:3, 257:258], flag_ap, ALU.mult)
        P.ts(hp[:, :, 0:3, 258:259], hp[:, :, 1:4, 2:3], flag_ap, ALU.mult)
        import os as _os
        SUB = int(_os.environ.get('SUB', '9'))
        SKEW = 1
        if SUB < 1:
            return
        convw = carve(misc, 0, [24, 3])
        convb = carve(misc, 72, [24])
        dtb = carve(misc, 96, [64])
        aneg = carve(misc, 160, [64])
        ddb = carve(misc, 224, [32])
        mngt = carve(misc, 256, [512])
        ssy = carve(misc, 768, [8])
        stg = [carve(misc, 832, [512]), carve(misc, 1344, [512])]
        P.dma(convw, m_convw, key="mc0")
        P.dma(convb, m_convb, key="mc1")
        P.dma(dtb, m_dtb, key="mc2")
        P.dma(aneg, m_alog, key="mc3")
        P.dma(ddb, m_dd, key="mc4")
        P.act(aneg, aneg, AF.Exp)
        P.ts(aneg, aneg, -1.0, ALU.mult)
        if SUB < 2:
            return
        dt = carve(arf, 0, [10, 64])
        ncum = carve(arf, 640, [10, 64])
        ecum = carve(arf, 1280, [10, 64])
        dtt = carve(arf, 1920, [10, 64])
        cdec = carve(arf, 2560, [10, 64])
        stt_ = carve(arf, 3840, [2, 512])
        z_tm = carve(arb, 0, [10, 512])
        x_tm = carve(arb, 5120, [10, 512])
        B_tm = carve(arb, 10240, [10, 128])
        BT = carve(arb, 11520, [NT])
        CT = carve(arb, 12800, [NT])
        hin = carve(arb, 14080, [2, 10, 512])
        yT = carve(arb, 24320, [4, NT])
        CBm_ = [carve(arb, 29440, [2, 128]), carve(arb, 29696, [2, 128])]
        yb = carve(arb, 29952, [512])
        ncp_ = [carve(arb, 30464, [2, 8]), carve(arb, 30480, [2, 8])]
        dsp = carve(arb, 30496, [2, 64])
        sqbf = sqb[:, :, :].rearrange("p a b -> p (a b)")
        xw = carve(sqbf, 0, [512])
        D2n_ = [carve(sqbf, 0, [2, 8, 128]), carve(sqbf, 2048, [2, 8, 128])]
        xdt_ = [carve(miscb, i * 512, [512]) for i in range(3)]
        EM_ = [carve(miscb, 1536 + i * 512, [4, 128]) for i in range(3)]
        mark('m dt')
        wv = wload(wsrc(wi, 5120, 64), 8, 64)
        for c in range(10):
            pb = bank()
            for k in range(8):
                P.mm(pb[:, 0:64], hch(k, c), wv[:, k, :], start=(k == 0), stop=(k == 7))
            t = tmp()
            P.tt(t[:, 0:64], pb[:, 0:64], dtb, ALU.add)
            P.act(t[:, 0:64], t[:, 0:64], AF.Exp)
            P.act(dt[:, c, :], t[:, 0:64], AF.Ln, bias=one_ap)
            if SUB < 3:
                continue
            td = tmp()
            P.tt(td[:, 0:64], dt[:, c, :], aneg, ALU.mult)
            P.copy(dsp[:, 0, :], td[:, 0:64])
            P.tt(td[:, 64:128], td[:, 0:64], dsp[:, 0, :], ALU.subtract)
            P.copy(dsp[:, 1, :], td[:, 64:128])
            if SUB < 4:
                continue
            pc = bank()
            for pi in range(2):
                P.mm(pc[:, 0:32], trifb, dsp[:, pi, 0:32], start=(pi == 0), stop=(pi == 1))
            for pi in range(2):
                P.mm(pc[:, 32:64], tribb, dsp[:, pi, 32:64], start=(pi == 0), stop=(pi == 1))
            for pi in range(2):
                P.mm(pc[:, 64:128], onesb, dsp[:, pi, :], start=(pi == 0), stop=(pi == 1))
            if SUB < 5:
                continue
            SUBX = _os.environ.get('SUBX', 'ta')
            if 't' in SUBX:
                P.ts(ncum[:, c, :], pc[:, 0:64], -1.0, ALU.mult)
            if 'a' in SUBX:
                P.act(ecum[:, c, :], pc[:, 0:64], AF.Exp)
            if SUB < 6:
                continue
            t2 = tmp()
            P.tt(t2[:, 0:64], pc[:, 64:128], ncum[:, c, :], ALU.add)
            P.act(t2[:, 0:64], t2[:, 0:64], AF.Exp)
            P.tt(dtt[:, c, :], t2[:, 0:64], dt[:, c, :], ALU.mult)
            P.act(cdec[:, c, :], pc[:, 64:128], AF.Exp)
        if mstop < 1:
            return
        for g in range(4):
            P.dma(stt_, ssm0[:, :, g * 512:(g + 1) * 512].rearrange("d n f -> n d f"), key="mc5")
            P.dma(mngt, m_ng[:, g * 512:(g + 1) * 512], key="mc6")
            mark('m g%d z' % g)
            wv = wload(wsrc(wi, g * 512, 512), 8, 512)
            for c in range(10):
                pb = bank()
                for k in range(8):
                    P.mm(pb[:, :], hch(k, c), wv[:, k, :], start=(k == 0), stop=(k == 7))
                P.act(z_tm[:, c, :], pb[:, :], AF.Silu)
            mark('m g%d conv' % g)
            wvx = wload(wsrc(wi, 2048 + g * 512, 512), 8, 512)
            s_, key = wslot()
            wvbc = s_[:, 0:8 * 256].rearrange("p (k n) -> p k n", n=256)
            P.dma(wvbc[:, :, 0:128], wsrc(wi, 4096 + g * 128, 128), key=key, queue='pool')
            P.dma(wvbc[:, :, 128:256], wsrc(wi, 4608 + g * 128, 128), key=key, queue='pool', join=True)
            xcs = [carve(sqbf, 512, [256]), carve(sqbf, 768, [256]), carve(sqbf, 1024, [256])]

            def conv_unit(q, s, ctr):
                cc = (g * 4 + q) if q < 4 else (16 + g if q == 4 else 20 + g)
                pb = bank()
                for k in range(8):
                    P.mm(pb[:, 0:260], (wvx[:, k, q * 128:(q + 1) * 128] if q < 4 else wvbc[:, k, (q - 4) * 128:(q - 3) * 128]),
                         hp[:, k, s, :], start=(k == 0), stop=(k == 7))
                yield
                t1 = tmp()
                P.act(t1[:, 0:256], pb[:, 2:258], AF.Identity, scale=convw[:, cc, 1:2], bias=convb[:, cc:cc + 1])
                yield
                P.stt(t1[:, 0:256], pb[:, 1:257], convw[:, cc, 0:1], t1[:, 0:256], ALU.mult, ALU.add)
                P.stt(t1[:, 0:256], pb[:, 3:259], convw[:, cc, 2:3], t1[:, 0:256], ALU.mult, ALU.add)
                yield
                r = ctr % 3
                if q < 4:
                    xc_ = xcs[r]
                    P.act(xc_, t1[:, 0:256], AF.Silu)
                    yield
                    for j in range(2):
                        P.transpose(ptb[:, r * 256 + j * 128:r * 256 + (j + 1) * 128], xc_[:, j * 128:(j + 1) * 128], identb)
                    yield
                    P.copy(x_tm[:, 2 * s:2 * s + 2, q * 128:(q + 1) * 128],
                           ptb[:, r * 256:(r + 1) * 256].rearrange("p (a b) -> p a b", a=2))
                elif q == 4:
                    P.act(BT[:, s * 256:(s + 1) * 256], t1[:, 0:256], AF.Silu)
                    yield
                    for j in range(2):
                        P.transpose(ptb[:, r * 256 + j * 128:r * 256 + (j + 1) * 128],
                                    BT[:, s * 256 + j * 128:s * 256 + (j + 1) * 128], identb)
                    yield
                    P.copy(B_tm[:, 2 * s:2 * s + 2, :], ptb[:, r * 256:(r + 1) * 256].rearrange("p (a b) -> p a b", a=2))
                else:
                    P.act(CT[:, s * 256:(s + 1) * 256], t1[:, 0:256], AF.Silu)

            cu = []
            for q in range(6):
                for s in range(5):
                    cu.append(conv_unit(q, s, len(cu)))
            S.tring = tmps + [rs, io[0], io[1]]
            run_pipelined(cu, skew=SKEW)
            S.tring = tmps
            if mstop < 2:
                continue
            mark('m g%d phA' % g)
            for d in range(2):
                order = (list(range(8)) + [8, 9]) if d == 0 else ([7, 6, 5, 4, 3, 2, 1, 0] + [9, 8])
                sv = stt_[:, d, :]
                sv3 = sv.rearrange("p (h q) -> p h q", q=64)
                for c in order:
                    if (d == 0 and c == 8) or (d == 1 and c == 9):
                        P.memset(sv, 0.0)
                    elif (d == 0 and c in (2, 4, 6)) or (d == 1 and c in (5, 3, 1)):
                        P.ts(sv, sv, flag_ap, ALU.mult)
                    P.copy(hin[:, d, c, :], sv, eng='act')
                    hc0 = d * 32 + g * 8
                    P.tt(xw.rearrange("p (h q) -> p h q", q=64), x_tm[:, c, :].rearrange("p (h q) -> p h q", q=64),
                         bc(dtt[:, c, hc0:hc0 + 8].unsqueeze(2), [128, 8, 64]), ALU.mult)
                    pS = bank()
                    P.mm(pS[:, :], B_tm[:, c, :], xw, start=True, stop=True)
                    P.tt(sv3, sv3, bc(cdec[:, c, hc0:hc0 + 8].unsqueeze(2), [128, 8, 64]), ALU.mult)
                    P.tt(sv, sv, pS[:, :], ALU.add)
                    if (d == 0 and c % 2 == 1) or (d == 1 and c % 2 == 0):
                        sg = stg[S.ti % 2]
                        S.ti += 1
                        P.copy(sg, sv, eng='act')
                        P.dma(ssm_out[c // 2, d, :, g * 512:(g + 1) * 512], sg, key="os%d" % (S.ti % 2))
            if mstop < 3:
                continue
            mark('m g%d phB' % g)
            rring = [psb[0], psb[1], psb[6]]
            v3 = lambda a: a.rearrange("p (h q) -> p h q", q=64)

            def head_unit(c):
                pcb = rring[S.rr % 3]
                S.rr += 1
                P.mm(pcb[:, 0:128], BT[:, c * 128:(c + 1) * 128], CT[:, c * 128:(c + 1) * 128], start=True, stop=True)
                yield
                CBm = CBm_[c % 2]
                P.copy(CBm[:, 0, :], pcb[:, 0:128])

            def quarter_unit(c, d, hb):
                hc0 = d * 32 + g * 8
                cd = c * 2 + d
                ncp = ncp_[cd % 2]
                D2n = D2n_[cd % 2]
                xdt = xdt_[cd % 3]
                CBm = CBm_[c % 2]
                if hb == 0:
                    tn = tmp()
                    P.copy(ncp[:, 0, :], ncum[:, c, hc0:hc0 + 8])
                    P.tt(tn[:, 0:8], ncum[:, c, hc0:hc0 + 8], ncp[:, 0, :], ALU.subtract)
                    P.copy(ncp[:, 1, :], tn[:, 0:8])
                yield
                if hb == 0:
                    for pi in range(2):
                        P.tt(D2n[:, pi], bc(identb.unsqueeze(1), [128, 8, 128]),
                             bc(ncp[:, pi, :].unsqueeze(2), [128, 8, 128]), ALU.mult)
                yield
                pr = rring[S.rr % 3]
                S.rr += 1
                prv = pr[:, :].rearrange("p (h i) -> p h i", h=4)
                for pi in range(2):
                    P.mm(prv, negonesb, D2n[:, pi, hb * 4:(hb + 1) * 4, :], start=(pi == 0), stop=False)
                for pi in range(2):
                    P.mm(prv, identb, bc(ncp[:, pi, hb * 4:(hb + 1) * 4].unsqueeze(2), [128, 4, 128]),
                         start=False, stop=False)
                P.mm(prv, identb, bc((nmfb if d == 0 else nmbb).unsqueeze(1), [128, 4, 128]), start=False, stop=True)
                yield
                EM = EM_[S.em % 3]
                S.em += 1
                P.act(EM.rearrange("p h i -> p (h i)"), pr[:, :], AF.Exp)
                if hb == 0:
                    P.tt(v3(xdt), v3(x_tm[:, c, :]), bc(dt[:, c, hc0:hc0 + 8].unsqueeze(2), [128, 8, 64]), ALU.mult)
                yield
                P.tt(EM, EM, bc(CBm[:, 0, :].unsqueeze(1), [128, 4, 128]), ALU.mult)
                yield
                py = psb[2 + d]
                for hh in range(4):
                    hl = hb * 4 + hh
                    P.mm(py[:, hl * 64:(hl + 1) * 64], EM[:, hh, :], xdt[:, hl * 64:(hl + 1) * 64], start=True, stop=True)

            def tail_unit(c):
                po0 = psb[4]
                po1 = psb[5]
                P.mm(po0[:, :], CT[:, c * 128:(c + 1) * 128], hin[:, 0, c, :], start=True, stop=True)
                P.mm(po1[:, :], CT[:, c * 128:(c + 1) * 128], hin[:, 1, c, :], start=True, stop=True)
                yield
                y1 = [rs, io[0]][c % 2]
                y2 = io[1]
                P.tt(v3(y1[:, :]), v3(po0[:, :]), bc(ecum[:, c, g * 8:g * 8 + 8].unsqueeze(2), [128, 8, 64]), ALU.mult)
                P.tt(v3(y2[:, :]), v3(po1[:, :]), bc(ecum[:, c, 32 + g * 8:32 + g * 8 + 8].unsqueeze(2), [128, 8, 64]), ALU.mult)
                P.tt(y1[:, :], y1[:, :], y2[:, :], ALU.add)
                yield
                P.tt(v3(y2[:, :]), v3(x_tm[:, c, :]), bc(ddb[:, g * 8:g * 8 + 8].unsqueeze(2), [128, 8, 64]), ALU.mult)
                P.tt(y1[:, :], y1[:, :], y2[:, :], ALU.add)
                for _ in range(5):
                    yield
                P.tt(y1[:, :], y1[:, :], psb[2][:, :], ALU.add)
                P.tt(y1[:, :], y1[:, :], psb[3][:, :], ALU.add)
                P.tt(y1[:, :], y1[:, :], z_tm[:, c, :], ALU.mult)
                yield
                P.act(y2[:, :], y1[:, :], AF.Square, accum_out=ssy[:, 0:1])
                P.act(ssy[:, 0:1], ssy[:, 0:1], AF.Sqrt, scale=1.0 / 512, bias=eps_ap)
                yield
                P.recip(ssy[:, 0:1], ssy[:, 0:1])
                P.stt(yb, y1[:, :], ssy[:, 0:1], mngt, ALU.mult, ALU.mult)
                yield
                for q in range(4):
                    P.transpose(ptb[:, q * 128:(q + 1) * 128], yb[:, q * 128:(q + 1) * 128], identb)
                yield
                P.copy(yT[:, :, c * 128:(c + 1) * 128], ptb[:, 0:512].rearrange("p (a b) -> p a b", a=4), eng='act')

            bu = []
            for c in range(10):
                bu.append(head_unit(c))
                for d in range(2):
                    for hb in range(2):
                        bu.append(quarter_unit(c, d, hb))
                bu.append(tail_unit(c))
            run_pipelined(bu, skew=1)
            if mstop < 4:
                continue
            mark('m g%d out' % g)
            wv = wload(wsrc(wo, 0, 1024, r0=g * 512, rows=512), 4, 1024)
            for m in range(8):
                for (t0, N) in TG:
                    pb = bank()
                    for kk in range(4):
                        P.mm(pb[:, :N], wv[:, kk, m * 128:(m + 1) * 128], yT[:, kk, t0:t0 + N],
                             start=(kk == 0), stop=(kk == 3))
                    resid_add(m, t0, N, pb, 1)

    mix_fns = [mixer_mamba, mixer_gmlp, mixer_fnet, mixer_attn]
    hp_dst = lambda k, t0, N: hT[:, :, :].rearrange("p k (s w) -> p k s w", w=260)[:, k, t0 // 256, 2:258]
    SEG = [(s * 256, 256) for s in range(5)]

    for li in range(nlayers):
        mark('L%d adaln' % li)
        if lite:
            if li == 0:
                for _ in xload_gen():
                    pass
            P.memset(modT[:], 0.25)
            P.memset(scT[:], 1.0)
            P.memset(gtT[:], 0.5)
        else:
            if li == 0:
                xg = xload_gen()
                for _ in adaln_gen(0):
                    next(xg, None)
                for _ in xg:
                    pass
            adaln_finish(li)
        mark('L%d ffn1' % li)
        if ffn:
            ffn_half(li, 0, 0, ss_next=bool(mixers[li % 4]))
        mark('L%d mixer' % li)
        kind = li % 4
        if mixers[kind]:
            if kind == 0:
                rmsnorm_mod(1, SEG, hp_dst)
            else:
                rmsnorm_mod(1, TG, hplain)
            mix_fns[kind]()
        mark('L%d ffn2' % li)
        nxt = (li + 1 < nlayers) and not lite
        if ffn:
            if nxt:
                S.nbank = 6
            ffn_half(li, 1, 2, adagen=(adaln_gen(li + 1) if nxt else None), ss_next=(li + 1 < nlayers))
            S.nbank = 7
        elif nxt:
            for _ in adaln_gen(li + 1):
                pass
    mark('output')
    for c in range(10):
        for kq in range(2):
            iob = io[kq]
            pb = bank()
            for kk in range(4):
                k = kq * 4 + kk
                P.transpose(pb[:, kk * 128:(kk + 1) * 128], xT[:, k, c * 128:(c + 1) * 128], identf)
            P.copy(iob[:, :], pb[:, :], eng=('act' if kq else 'dve'))
            P.dma(yout[c * 128:(c + 1) * 128, kq * 512:(kq + 1) * 512], iob[:], key="io%d" % kq)

    P.emit(st)
    st.close()
    S.P = P
    return nc, S


def _consts():
    i = np.arange(128)
    ident = np.eye(128, dtype=np.float32)
    tri_f = (i[:, None] <= i[None, :]).astype(np.float32)
    tri_b = (i[:, None] >= i[None, :]).astype(np.float32)
    ones = np.ones((128, 128), np.float32)
    cst_f = np.stack([ident, tri_f, tri_b, ones], axis=1)
    nm_f = np.where(i[:, None] > i[None, :], NEG, 0.0).astype(np.float32)
    nm_b = np.where(i[:, None] < i[None, :], NEG, 0.0).astype(np.float32)
    cst_b = np.stack([ident, tri_f, tri_b, ones, -ones, nm_f, nm_b], axis=1).astype(NPBF)
    n = np.arange(256)
    ang = 2 * np.pi * np.outer(n, n) / 256.0
    cs = np.concatenate([np.cos(ang), np.sin(ang)], axis=1) / 16.0
    dftc = cs.reshape(2, 128, 512).transpose(1, 0, 2).astype(NPBF)

    def pos(L):
        m = np.arange(L)
        a = 2 * np.pi * np.outer(m, m) / float(L)
        return np.cos(a) / np.sqrt(L), -np.sin(a) / np.sqrt(L)
    c1024, s1024 = pos(1024)
    c256, s256 = pos(256)
    fullA = np.stack([c1024, s1024], axis=1)
    blkc = np.zeros((1024, 1024)); blks = np.zeros((1024, 1024))
    for s in range(4):
        blkc[s * 256:(s + 1) * 256, s * 256:(s + 1) * 256] = c256
        blks[s * 256:(s + 1) * 256, s * 256:(s + 1) * 256] = s256
    blkA = np.stack([blkc, blks], axis=1)
    toA = lambda a: a.reshape(8, 128, 2, 1024).transpose(1, 0, 2, 3).astype(NPBF)
    dftpA_s = toA(fullA)
    dftpA_p = toA(blkA)
    dftpB = np.stack([c256, s256], axis=1).reshape(2, 128, 2, 256).transpose(1, 0, 2, 3).astype(NPBF)
    am_s = np.zeros((10, 128, 1024), np.float32)
    am_p = np.full((1280, 1024), NEG, np.float32)
    for s in range(4):
        am_p[256 + s * 256:256 + (s + 1) * 256, s * 256:(s + 1) * 256] = 0.0
    am_p = am_p.reshape(10, 128, 1024)
    t = np.arange(1024)
    inv = 10000.0 ** (-np.arange(16, dtype=np.float32) / 16.0)
    angr = (t // 64).astype(np.float32)[:, None] * inv[None, :]
    angc = (t % 64).astype(np.float32)[:, None] * inv[None, :]
    cosT = np.ones((1280, 32), np.float32)
    sinT = np.zeros((1280, 32), np.float32)
    cosT[:1024, :16] = np.cos(angr); cosT[:1024, 16:] = np.cos(angc)
    sinT[:1024, :16] = np.sin(angr); sinT[:1024, 16:] = np.sin(angc)
    tot = lambda a: np.ascontiguousarray(a.reshape(10, 128, 32).transpose(1, 0, 2))
    rope_s = (tot(cosT), tot(sinT))
    rope_p = (tot(np.ones((1280, 32), np.float32)), tot(np.zeros((1280, 32), np.float32)))
    return dict(cst_f=np.ascontiguousarray(cst_f), cst_b=np.ascontiguousarray(cst_b), dftc=np.ascontiguousarray(dftc),
                dftpA_s=np.ascontiguousarray(dftpA_s), dftpA_p=np.ascontiguousarray(dftpA_p),
                dftpB=np.ascontiguousarray(dftpB), am_s=am_s.astype(NPBF), am_p=am_p.astype(NPBF),
                rope_s=rope_s, rope_p=rope_p)


def _pp(v, chunks):
    return np.ascontiguousarray(np.asarray(v, np.float32).reshape(chunks, 128).T)


def _rep(v):
    v = np.asarray(v, np.float32).reshape(1, -1)
    return np.ascontiguousarray(np.broadcast_to(v, (128, v.shape[1])))


def make_in_maps(inp):
    C = _consts()
    f = lambda a: np.ascontiguousarray(np.asarray(a, np.float32))
    shared = {
        "dftpB": C["dftpB"], "dftc": C["dftc"], "cst_f": C["cst_f"], "cst_b": C["cst_b"],
        "ln_gT": np.ascontiguousarray(f(inp["ln_g"]).reshape(4, 3, 8, 128).transpose(3, 0, 1, 2)),
        "ada_bT": np.ascontiguousarray(f(inp["ada_b"]).reshape(4, 72, 128).transpose(2, 0, 1)),
        "ada_w": f(inp["ada_w"]),
        "ff1_w_in": f(inp["ff1_w_in"]), "ff2_w_in": f(inp["ff2_w_in"]),
        "ff1_w_out": f(inp["ff1_w_out"]), "ff2_w_out": f(inp["ff2_w_out"]),
        "m_w_in": f(inp["m_w_in"]),
        "m_convw": np.ascontiguousarray(f(inp["m_conv_w"])[0].reshape(3, 24, 128).transpose(2, 1, 0)),
        "m_convb": _pp(f(inp["m_conv_b"])[0], 24),
        "m_dtb": _rep(f(inp["m_dt_bias"])[0].reshape(-1)),
        "m_alog": _rep(f(inp["m_a_log"])[0].reshape(-1)),
        "m_dd": _rep(f(inp["m_d"])[0]),
        "m_ng": _rep(f(inp["m_norm_g"])[0]),
        "m_w_out": f(inp["m_w_out"]),
        "g_w_in": f(inp["g_w_in"]),
        "g_buT": _pp(f(inp["g_b_in"])[0, :2048], 16),
        "g_bv": np.ascontiguousarray(f(inp["g_b_in"])[0:1, 2048:]),
        "g_ng": _rep(f(inp["g_norm_g"])[0]),
        "g_w_s": f(inp["g_w_s"]),
        "g_b_s": np.ascontiguousarray(f(inp["g_b_s"]).reshape(1, 1024)),
        "g_w_out": f(inp["g_w_out"]),
        "f_w_out": f(inp["f_w_out"]),
        "f_bT": _pp(f(inp["f_b_out"])[0], 8),
        "a_w_qkv": f(inp["a_w_qkv"]),
        "a_qg": _rep(f(inp["a_q_norm"])[0]),
        "a_kg": _rep(f(inp["a_k_norm"])[0]),
        "a_w_o": f(inp["a_w_o"]),
    }
    xp = f(inp["x_prompt"]); xs = f(inp["x_sample"])
    c = f(inp["c"]); cctx = f(inp["c_ctx"])
    maps = []
    for core in range(8):
        m = dict(shared)
        if core < 6:
            seqs = list(range(5 * core, 5 * core + 5))
            xin = xp[seqs].reshape(NT, D)
            cond = np.stack([cctx, cctx], 0)
            m["flag"] = np.zeros((128, 1), np.float32)
            m["ssm0"] = np.zeros((2, 128, 2048), np.float32)
            m["ck"] = np.zeros((256, 256), np.float32)
            m["cv"] = np.zeros((256, 256), np.float32)
            m["amask"] = C["am_p"]; m["ropec"], m["ropes"] = C["rope_p"]
            m["dftpA"] = C["dftpA_p"]
        else:
            b = core - 6
            xin = np.concatenate([xs[b], xp[30 + b]], 0)
            cond = np.stack([c[b], cctx], 0)
            m["flag"] = np.ones((128, 1), np.float32)
            s0 = f(inp["state_ssm"])[b, 0]
            m["ssm0"] = np.ascontiguousarray(s0.reshape(2, 2048, 128).transpose(0, 2, 1))
            m["ck"] = np.ascontiguousarray(f(inp["cache_k"])[b, 0].reshape(256, 256))
            m["cv"] = np.ascontiguousarray(f(inp["cache_v"])[b, 0].reshape(256, 256))
            m["amask"] = C["am_s"]; m["ropec"], m["ropes"] = C["rope_s"]
            m["dftpA"] = C["dftpA_s"]
        m["xin"] = np.ascontiguousarray(xin)
        m["condT"] = np.ascontiguousarray(cond.reshape(2, 8, 128).transpose(2, 1, 0))
        maps.append(m)
    return maps


def assemble(results):
    yp = np.zeros((32, 256, D), np.float32)
    ys = np.zeros((2, 1024, D), np.float32)
    ssm = np.zeros((32, 1, 2, 32, 64, 128), np.float32)
    nk = np.zeros((32, 1, 256, 4, 64), np.float32)
    nv = np.zeros((32, 1, 256, 4, 64), np.float32)
    for core in range(8):
        r = results[core]
        y = np.asarray(r["yout"]).reshape(5, 256, D)
        so = np.asarray(r["ssm_out"]).reshape(5, 2, 128, 32, 64).transpose(0, 1, 3, 4, 2)
        k = np.asarray(r["newk"]).reshape(5, 256, 4, 64)
        v = np.asarray(r["newv"]).reshape(5, 256, 4, 64)
        if core < 6:
            sl = slice(5 * core, 5 * core + 5)
            yp[sl] = y; ssm[sl, 0] = so; nk[sl, 0] = k; nv[sl, 0] = v
        else:
            b = core - 6
            ys[b] = y[:4].reshape(1024, D)
            yp[30 + b] = y[4]; ssm[30 + b, 0] = so[4]; nk[30 + b, 0] = k[4]; nv[30 + b, 0] = v[4]
    return yp, ys, ssm, nk, nv


_CACHE = {}


def kernel(**inputs):
    if "nc" not in _CACHE:
        _CACHE["nc"] = build()[0]
    nc = _CACHE["nc"]
    maps = make_in_maps(inputs)
    res = run_bass_kernel_spmd(nc, maps, core_ids=list(range(8)))
    return assemble(res.results)
```

```python
import bisect
from contextlib import ExitStack

import numpy as np
import ml_dtypes
import concourse.bass as bass
import concourse.mybir as mybir
from concourse.bass_utils import run_bass_kernel_spmd

F32 = mybir.dt.float32
BF16 = mybir.dt.bfloat16
AF = mybir.ActivationFunctionType
ALU = mybir.AluOpType
AX = mybir.AxisListType
_ISZ = {F32: 4, BF16: 2}
NPBF = ml_dtypes.bfloat16


def _region(ap):
    sp = str(ap.space).upper()
    if 'DRAM' in sp or 'HBM' in sp:
        return None
    t = ap.tensor
    pp = 1
    for s in list(t.shape)[1:]:
        pp *= int(s)
    isz = _ISZ.get(ap.dtype, 4)
    off = int(ap.offset)
    dims = list(ap.ap)
    p0 = off // pp
    f0 = off % pp
    pstep, pcnt = int(dims[0][0]), int(dims[0][1])
    if pstep == 0:
        pcnt = 1
    ext = 0
    for st, cn in dims[1:]:
        ext += abs(int(st)) * (int(cn) - 1)
    if 'PSUM' in sp:
        return (t.name, 0, 128, 0, pp * isz)
    return (t.name, p0, p0 + pcnt, f0 * isz, (f0 + ext + 1) * isz)


class Op:
    __slots__ = ('eng', 'fn', 'deps', 'idx', 'signal', 'sigcount', 'dma_key', 'dma_cnt', 'is_dma', 'waits')

    def __init__(self, eng, fn):
        self.eng = eng
        self.fn = fn
        self.deps = set()
        self.signal = False
        self.sigcount = 0
        self.is_dma = False
        self.dma_key = None
        self.dma_cnt = 0
        self.waits = []


class Prog:
    ENGS = ('pe', 'act', 'dve', 'pool', 'sp')

    def __init__(self, nc):
        self.nc = nc
        self.ops = []
        self.acc = {}
        self.dma_state = {}
        self.dma_keys = []

    def _touch(self, op, aps_r, aps_w):
        idx = op.idx
        ops = self.ops
        for is_w, aps in ((False, aps_r), (True, aps_w)):
            for ap in aps:
                if ap is None or isinstance(ap, (int, float)):
                    continue
                r = _region(ap)
                if r is None:
                    continue
                name, p0, p1, lo, hi = r
                is_psum = 'PSUM' in str(ap.space).upper()
                lst = self.acc.get(name, [])
                keep = []
                for rec in lst:
                    q0, q1, l2, h2, w2, oi = rec
                    if oi == idx:
                        keep.append(rec)
                        continue
                    overlap = (q0 < p1 and p0 < q1 and l2 < hi and lo < h2)
                    if overlap and (is_w or w2):
                        op.deps.add(oi)
                    elif overlap and is_psum and ops[oi].eng != op.eng:
                        op.deps.add(oi)
                    contained = (q0 >= p0 and q1 <= p1 and l2 >= lo and h2 <= hi)
                    if contained:
                        o2 = ops[oi]
                        same_stream = (o2.is_dma and op.is_dma and o2.dma_key == op.dma_key) or \
                                      ((not o2.is_dma) and (not op.is_dma) and o2.eng == op.eng)
                        if is_w or (same_stream and w2 == is_w):
                            continue
                    keep.append(rec)
                keep.append((p0, p1, lo, hi, is_w, idx))
                self.acc[name] = keep

    def add(self, eng, fn, reads=(), writes=()):
        op = Op(eng, fn)
        op.idx = len(self.ops)
        self.ops.append(op)
        self._touch(op, reads, writes)
        return op

    def dma(self, out, in_, key, queue='sp', join=False):
        key = queue + '_' + str(key)
        op = Op(queue, None)
        op.idx = len(self.ops)
        op.is_dma = True
        op.dma_key = key
        st = self.dma_state.get(key)
        if st is None:
            st = [0, None]
            self.dma_state[key] = st
            self.dma_keys.append(key)
        if st[1] is not None and not join:
            op.deps.add(st[1])
        st[0] += 16
        op.dma_cnt = st[0]
        st[1] = op.idx
        op.fn = lambda e, o=out, i=in_: e.dma_start(out=o, in_=i)
        self.ops.append(op)
        self._touch(op, [in_], [out])
        return op

    def mm(self, out, lhsT, rhs, start=True, stop=True):
        return self.add('pe', lambda e: e.matmul(out, lhsT, rhs, start=start, stop=stop),
                        [lhsT, rhs] + ([] if start else [out]), [out])

    def transpose(self, out, in_, ident):
        return self.add('pe', lambda e: e.transpose(out, in_, ident), [in_, ident], [out])

    def act(self, out, in_, func, scale=1.0, bias=None, accum_out=None):
        kw = {}
        if bias is not None:
            kw['bias'] = bias
        if accum_out is not None:
            kw['accum_out'] = accum_out
        rd = [in_, scale, bias]
        return self.add('act', lambda e: e.activation(out=out, in_=in_, func=func, scale=scale, **kw),
                        rd, [out, accum_out])

    def tt(self, out, in0, in1, op, eng='dve'):
        return self.add(eng, lambda e: e.tensor_tensor(out=out, in0=in0, in1=in1, op=op), [in0, in1], [out])

    def ts(self, out, in0, s1, op0, s2=None, op1=None, eng='dve'):
        kw = {}
        if op1 is not None:
            kw['op1'] = op1
        return self.add(eng, lambda e: e.tensor_scalar(out=out, in0=in0, scalar1=s1, scalar2=s2, op0=op0, **kw),
                        [in0, s1, s2], [out])

    def stt(self, out, in0, scalar, in1, op0, op1):
        return self.add('dve', lambda e: e.scalar_tensor_tensor(out=out, in0=in0, scalar=scalar, in1=in1,
                                                                op0=op0, op1=op1), [in0, in1, scalar], [out])

    def copy(self, out, in_, eng='dve'):
        if eng == 'act':
            return self.add('act', lambda e: e.copy(out=out, in_=in_), [in_], [out])
        return self.add(eng, lambda e: e.tensor_copy(out=out, in_=in_), [in_], [out])

    def memset(self, ap, val, eng='dve'):
        return self.add(eng, lambda e: e.memset(ap, val), [], [ap])

    def reduce(self, out, in_, op, axis=AX.X):
        return self.add('dve', lambda e: e.tensor_reduce(out=out, in_=in_, op=op, axis=axis), [in_], [out])

    def recip(self, out, in_):
        return self.add('dve', lambda e: e.reciprocal(out=out, in_=in_), [in_], [out])

    def emit(self, stack):
        nc = self.nc
        ops = self.ops
        for op in ops:
            for d in op.deps:
                o2 = ops[d]
                if not o2.is_dma:
                    if o2.eng == 'pe' and op.eng == 'pe' and not op.is_dma:
                        continue
                    o2.signal = True
        cnt = {e: 0 for e in self.ENGS}
        for op in ops:
            if op.is_dma:
                continue
            if op.signal:
                cnt[op.eng] += 1
            op.sigcount = cnt[op.eng]
        key_hist = {}
        for op in ops:
            if op.is_dma:
                key_hist.setdefault(op.dma_key, []).append((op.idx, op.dma_cnt))
        waited = {e: {} for e in self.ENGS}
        for op in ops:
            need = {}
            for d in op.deps:
                o2 = ops[d]
                if o2.is_dma:
                    h = key_hist[o2.dma_key]
                    pos = bisect.bisect_left(h, (op.idx, 0)) - 1
                    c = h[pos][1] if pos >= 0 else o2.dma_cnt
                    if op.is_dma and op.dma_key == o2.dma_key:
                        c = o2.dma_cnt
                    c = max(c, o2.dma_cnt)
                    k = ('dma', o2.dma_key)
                    need[k] = max(need.get(k, 0), c)
                else:
                    if o2.eng == 'pe' and op.eng == 'pe' and not op.is_dma:
                        continue
                    k = ('eng', o2.eng)
                    need[k] = max(need.get(k, 0), o2.sigcount)
            w = waited[op.eng]
            for k, v in need.items():
                if w.get(k, 0) >= v:
                    continue
                w[k] = v
                op.waits.append((k, v))
        final_waits = [(('dma', k), self.dma_state[k][0]) for k in self.dma_keys]
        sems = {}
        for e in self.ENGS:
            sems[('eng', e)] = stack.enter_context(nc.semaphore('s_' + e))
        for k in self.dma_keys:
            sems[('dma', k)] = stack.enter_context(nc.semaphore('d_' + str(k)))
        self.n_sems = len(sems)
        self.sig_counts = cnt
        block = stack.enter_context(nc.Block())
        byeng = {e: [op for op in ops if op.eng == e] for e in self.ENGS}

        def run(e, engname):
            for op in byeng[engname]:
                for k, v in op.waits:
                    e.wait_ge(sems[k], v)
                ins = op.fn(e)
                if op.is_dma:
                    ins.then_inc(sems[('dma', op.dma_key)], 16)
                elif op.signal:
                    ins.then_inc(sems[('eng', engname)], 1)
            if engname == 'sp':
                for k, v in final_waits:
                    e.wait_ge(sems[k], v)

        @block.tensor
        def _(e):
            run(e, 'pe')

        @block.scalar
        def _(e):
            run(e, 'act')

        @block.vector
        def _(e):
            run(e, 'dve')

        @block.gpsimd
        def _(e):
            run(e, 'pool')

        @block.sync
        def _(e):
            run(e, 'sp')


D = 1024
NT = 1280
FH = 2816
NSLOT = 3
SLOT = 4096
ARB = 30720
ARF = 5120
TG = [(0, 512), (512, 512), (1024, 256)]
EPS = 1e-6
NEG = -30000.0


def cidx(t0):
    return 0 if t0 < 1024 else 1


def bc(ap, shape):
    return ap.to_broadcast(list(shape))


class K:
    pass


def run_pipelined(gens, skew=1):
    active = []
    gens = list(gens)
    i = 0
    while i < len(gens) or active:
        if i < len(gens):
            active.append([gens[i], 0])
            i += 1
        for a in list(active):
            if a[1] % skew == 0:
                try:
                    next(a[0])
                except StopIteration:
                    active.remove(a)
                    continue
            a[1] += 1


def build(nlayers=4, mixers=(True, True, True, True), ffn=True, mstop=99, lite=False):
    nc = bass.Bass("TRN2", target_bir_lowering=False)
    st = ExitStack()
    S = K()

    S.declared = set()
    BIG = {"ada_w": None, "ff1_w_in": None, "ff2_w_in": None, "ff1_w_out": None, "ff2_w_out": None,
           "m_w_in": 0, "m_w_out": 0, "g_w_in": 1, "g_w_out": 1, "f_w_out": 2, "a_w_qkv": 3, "a_w_o": 3}

    def din(name, shape, dt=F32):
        if lite and name in BIG:
            kind = BIG[name]
            if kind is None or not (mixers[kind] and nlayers > kind):
                return None
        S.declared.add(name)
        return nc.dram_tensor(name, list(shape), dt, kind="ExternalInput").ap()

    def dout(name, shape, dt=F32):
        return nc.dram_tensor(name, list(shape), dt, kind="ExternalOutput").ap()

    def sb(name, shape, dt=F32):
        return st.enter_context(nc.sbuf_tensor(name, list(shape), dt))

    xin = din("xin", [NT, D])
    condT = din("condT", [128, 8, 2])
    flag_d = din("flag", [128, 1])
    ssm0 = din("ssm0", [2, 128, 2048])
    ck_d = din("ck", [256, 256])
    cv_d = din("cv", [256, 256])
    amask_d = din("amask", [10, 128, 1024], BF16)
    ropec_d = din("ropec", [128, 10, 32])
    ropes_d = din("ropes", [128, 10, 32])
    dftpA_d = din("dftpA", [128, 8, 2, 1024], BF16)
    dftpB_d = din("dftpB", [128, 2, 2, 256], BF16)
    dftc_d = din("dftc", [128, 2, 512], BF16)
    cst_f = din("cst_f", [128, 4, 128])
    cst_b = din("cst_b", [128, 7, 128], BF16)
    ln_gT = din("ln_gT", [128, 4, 3, 8])
    ada_bT = din("ada_bT", [128, 4, 72])
    ada_w = din("ada_w", [4, D, 9 * D])
    ff_w_in = [din("ff1_w_in", [4, D, 2 * FH]), din("ff2_w_in", [4, D, 2 * FH])]
    ff_w_out = [din("ff1_w_out", [4, FH, D]), din("ff2_w_out", [4, FH, D])]
    m_w_in = din("m_w_in", [1, D, 5184])
    m_convw = din("m_convw", [128, 24, 3])
    m_convb = din("m_convb", [128, 24])
    m_dtb = din("m_dtb", [128, 64])
    m_alog = din("m_alog", [128, 64])
    m_dd = din("m_dd", [128, 32])
    m_ng = din("m_ng", [128, 2048])
    m_w_out = din("m_w_out", [1, 2048, D])
    g_w_in = din("g_w_in", [1, D, 4096])
    g_buT = din("g_buT", [128, 16])
    g_bv = din("g_bv", [1, 2048])
    g_ng = din("g_ng", [128, 2048])
    g_w_s = din("g_w_s", [1, 8, 128, 128])
    g_b_s = din("g_b_s", [1, 1024])
    g_w_out = din("g_w_out", [1, 2048, D])
    f_w_out = din("f_w_out", [1, D, D])
    f_bT = din("f_bT", [128, 8])
    a_w_qkv = din("a_w_qkv", [1, D, 1536])
    a_qg = din("a_qg", [128, 64])
    a_kg = din("a_kg", [128, 64])
    a_w_o = din("a_w_o", [1, D, D])

    yout = dout("yout", [NT, D])
    ssm_out = dout("ssm_out", [5, 2, 128, 2048])
    newk = dout("newk", [NT, 256])
    newv = dout("newv", [NT, 256])

    xT = sb("xT", [128, 8, NT])
    hT = sb("hT", [128, 8, 1300], BF16)
    arb = sb("arb", [128, ARB], BF16)
    arf = sb("arf", [128, ARF])
    slots = [sb("slot%d" % i, [128, SLOT], BF16) for i in range(NSLOT)]
    cf = sb("cf", [128, 4, 128])
    cb = sb("cb", [128, 7, 128], BF16)
    sqb = sb("sqb", [128, 8, 512], BF16)
    tmps = [sb("tmp%d" % i, [128, 512]) for i in range(4)]
    io = [sb("io%d" % i, [128, 512]) for i in range(2)]
    rs = sb("rs", [128, 512])
    modT = sb("modT", [128, 72, 2])
    scT = sb("scT", [128, 3, 8, 2])
    gtT = sb("gtT", [128, 3, 8, 2])
    lng = sb("lng", [128, 4, 3, 8])
    adab = sb("adab", [128, 4, 72])
    cnd = sb("cnd", [128, 8, 2])
    siluc = sb("siluc", [128, 8, 2], BF16)
    small = sb("small", [128, 64])
    misc = sb("misc", [128, 1856])
    miscb = sb("miscb", [128, 3072], BF16)
    psb = [st.enter_context(nc.psum_tensor("ps%d" % i, [128, 512], F32)) for i in range(7)]
    ptb = st.enter_context(nc.psum_tensor("ptb", [128, 1024], BF16))

    P = Prog(nc)
    S.marks = []

    def mark(label):
        S.marks.append((label, sum(1 for o in P.ops if o.eng == 'pe')))
    S.ti = 0
    S.si = 0
    S.bi = 0
    S.rr = 0
    S.em = 0

    S.tring = tmps

    def tmp():
        S.ti += 1
        return S.tring[S.ti % len(S.tring)]

    S.nbank = 7

    def bank(lo=0, hi=None):
        hi = S.nbank if hi is None else hi
        S.bi += 1
        return psb[lo + (S.bi % (hi - lo))]

    identf = cf[:, 0, :]
    trif = cf[:, 1, :]
    trib = cf[:, 2, :]
    onesf = cf[:, 3, :]
    identb = cb[:, 0, :]
    trifb = cb[:, 1, :]
    tribb = cb[:, 2, :]
    onesb = cb[:, 3, :]
    negonesb = cb[:, 4, :]
    nmfb = cb[:, 5, :]
    nmbb = cb[:, 6, :]
    eps_ap = small[:, 0:1]
    one_ap = small[:, 1:2]
    flag_ap = small[:, 2:3]

    def carve(t, off, shape):
        n = 1
        for s in shape:
            n *= s
        v = t[:, off:off + n]
        if len(shape) == 1:
            return v
        names = " ".join("d%d" % i for i in range(len(shape)))
        kw = {"d%d" % i: shape[i] for i in range(1, len(shape))}
        return v.rearrange("p (%s) -> p %s" % (names, names), **kw)

    def wslot():
        s = slots[S.si % NSLOT]
        key = "w%d" % (S.si % NSLOT)
        S.si += 1
        return s, key

    def wload(src, kt, ncols, parts=128):
        s, key = wslot()
        v = s[0:parts, 0:kt * ncols].rearrange("p (k n) -> p k n", n=ncols)
        P.dma(v, src, key=key, queue='pool')
        return v

    def wsrc(w2d, c0, ncols, r0=0, rows=None, p=128):
        rows = rows if rows is not None else w2d.shape[0]
        return w2d[r0:r0 + rows, c0:c0 + ncols].rearrange("(k p) n -> p k n", p=p)

    P.dma(cf[:], cst_f, key="c0")
    P.dma(cb[:], cst_b, key="c1")
    P.dma(lng[:], ln_gT, key="c2")
    P.dma(adab[:], ada_bT, key="c3")
    P.dma(cnd[:], condT, key="c4")
    P.dma(small[:, 2:3], flag_d, key="c5")
    P.memset(small[:, 0:1], EPS)
    P.memset(small[:, 1:2], 1.0)
    P.act(siluc[:], cnd[:], AF.Silu)

    def xload_gen():
        for c in range(10):
            for kq in range(2):
                iob = io[kq]
                P.dma(iob[:], xin[c * 128:(c + 1) * 128, kq * 512:(kq + 1) * 512], key="io%d" % kq)
                pb = bank(0, 6)
                for kk in range(4):
                    P.transpose(pb[:, kk * 128:(kk + 1) * 128], iob[:, kk * 128:(kk + 1) * 128], identf)
                P.copy(xT[:, kq * 4:(kq + 1) * 4, c * 128:(c + 1) * 128],
                       pb[:, :].rearrange("p (a b) -> p a b", a=4), eng=('act' if kq else 'dve'))
            yield

    def adaln_gen(li):
        pm = psb[6]
        aw = ada_w[li]
        for blk in range(18):
            wv = wload(wsrc(aw, blk * 512, 512), 8, 512)
            for jj in range(4):
                j = blk * 4 + jj
                for k in range(8):
                    P.mm(pm[:, 2 * j:2 * j + 2], wv[:, k, jj * 128:(jj + 1) * 128], siluc[:, k, :],
                         start=(k == 0), stop=(k == 7))
            yield

    def adaln_finish(li):
        pm = psb[6]
        P.tt(modT[:], pm[:, 0:144].rearrange("p (j c) -> p j c", c=2),
             bc(adab[:, li, :].unsqueeze(2), [128, 72, 2]), ALU.add)
        for n in range(3):
            t = tmp()
            tv = t[:, 0:16].rearrange("p (k c) -> p k c", c=2)
            P.ts(tv, modT[:, (3 * n + 1) * 8:(3 * n + 2) * 8, :], 1.0, ALU.add)
            P.tt(scT[:, n], tv, bc(lng[:, li, n, :].unsqueeze(2), [128, 8, 2]), ALU.mult)
            P.ts(gtT[:, n], modT[:, (3 * n + 2) * 8:(3 * n + 3) * 8, :], (1.0 if n == 1 else 0.5), ALU.mult)

    def sh_ap(n, k, ci):
        return modT[:, 3 * n * 8 + k, ci:ci + 1]

    S.ss = None

    def rmsnorm_mod(n, groups, dst):
        ss = S.ss
        S.ss = None
        for (t0, N) in groups:
            ci = cidx(t0)
            if ss is None:
                for k in range(8):
                    P.act(sqb[:, k, :N], xT[:, k, t0:t0 + N], AF.Square)
                pb = bank()
                for k in range(8):
                    P.mm(pb[:, :N], onesb, sqb[:, k, :N], start=(k == 0), stop=(k == 7))
                ssv = pb[:, :N]
            else:
                c0 = t0 % 512
                ssv = ss[t0 // 512][:, c0:c0 + N]
            P.act(rs[:, :N], ssv, AF.Sqrt, scale=1.0 / D, bias=eps_ap)
            P.recip(rs[:, :N], rs[:, :N])
            for k in range(8):
                t = tmp()
                P.tt(t[:, :N], xT[:, k, t0:t0 + N], rs[:, :N], ALU.mult)
                P.act(dst(k, t0, N), t[:, :N], AF.Identity, scale=scT[:, n, k, ci:ci + 1], bias=sh_ap(n, k, ci))

    def hplain(k, t0, N):
        return hT[:, k, t0:t0 + N]

    def resid_add(m, t0, N, pb, n):
        ci = cidx(t0)
        P.stt(xT[:, m, t0:t0 + N], pb[:, :N], gtT[:, n, m, ci:ci + 1], xT[:, m, t0:t0 + N], ALU.mult, ALU.add)

    def ffn_half(li, which, n, adagen=None, ss_next=True):
        rmsnorm_mod(n, TG, hplain)
        hid = carve(arb, 0, [22, NT])
        w_in = ff_w_in[which][li]
        w_out = ff_w_out[which][li]
        for jb in range(11):
            s, key = wslot()
            wv = s[:, 0:8 * 512].rearrange("p (k n) -> p k n", n=512)
            P.dma(wv[:, :, 0:256], wsrc(w_in, jb * 256, 256), key=key, queue='pool')
            P.dma(wv[:, :, 256:512], wsrc(w_in, FH + jb * 256, 256), key=key, queue='pool', join=True)
            for (t0, N) in TG:
                for half in range(2):
                    pg = bank()
                    pu = bank()
                    for k in range(8):
                        P.mm(pg[:, :N], wv[:, k, half * 128:(half + 1) * 128], hT[:, k, t0:t0 + N],
                             start=(k == 0), stop=(k == 7))
                    for k in range(8):
                        P.mm(pu[:, :N], wv[:, k, 256 + half * 128:256 + (half + 1) * 128], hT[:, k, t0:t0 + N],
                             start=(k == 0), stop=(k == 7))
                    t = tmp()
                    P.act(t[:, :N], pg[:, :N], AF.Silu)
                    P.tt(hid[:, jb * 2 + half, t0:t0 + N], t[:, :N], pu[:, :N], ALU.mult)
            if adagen is not None:
                next(adagen, None)
        nb_save = S.nbank
        ssb = [psb[3], psb[4], psb[5]]
        if ss_next:
            S.nbank = 3
        pend = []

        def ss_update(m, gi, t0, N):
            sq = sqb[:, (m * 3 + gi) % 8, :N]
            P.act(sq, xT[:, m, t0:t0 + N], AF.Square)
            P.mm(ssb[gi][:, :N], onesb, sq, start=(m == 0), stop=(m == 7))

        for m in range(8):
            wv = wload(wsrc(w_out, m * 128, 128), 22, 128)
            for gi, (t0, N) in enumerate(TG):
                pb = bank()
                for kk in range(22):
                    P.mm(pb[:, :N], wv[:, kk, :], hid[:, kk, t0:t0 + N], start=(kk == 0), stop=(kk == 21))
                resid_add(m, t0, N, pb, n)
                if ss_next:
                    pend.append((m, gi, t0, N))
                    if len(pend) > 2:
                        ss_update(*pend.pop(0))
            if adagen is not None:
                next(adagen, None)
        for a in pend:
            ss_update(*a)
        S.nbank = nb_save
        if ss_next:
            S.ss = ssb
        if adagen is not None:
            for _ in adagen:
                pass

    def mixer_gmlp():
        wi = g_w_in[0]
        wo = g_w_out[0]
        gng = carve(arf, 128, [2048])
        P.dma(gng, g_ng, key="mc0")
        buT = carve(arf, 0, [16])
        P.dma(buT, g_buT, key="mc1")
        ssq = carve(arf, 16, [10, 4])
        rst = carve(arf, 64, [16])
        P.memset(miscb[:, :], 0.0)
        bvrow = miscb[0:1, 0:2048]
        bsrow = miscb[0:1, 2048:3072]
        P.dma(bvrow, g_bv, key="mc2", queue='pool')
        P.dma(bsrow, g_b_s, key="mc3", queue='pool')
        bvall = miscb[:, 0:2048]
        bsall = miscb[:, 2048:3072]
        wsn = carve(arb, 0, [8, 128])
        wsT = carve(arb, 1024, [8, 128])
        P.dma(wsn, g_w_s[0].rearrange("h i j -> i h j"), key="mc4", queue='pool')
        for hq in range(2):
            for hh in range(4):
                h = hq * 4 + hh
                P.transpose(ptb[:, hh * 128:(hh + 1) * 128], wsn[:, h, :], identb)
            P.copy(wsT[:, hq * 4:(hq + 1) * 4, :], ptb[:, 0:512].rearrange("p (a b) -> p a b", a=4))
        v_tm = carve(arb, 2048, [5, 2048])
        svT = carve(arb, 2048 + 10240, [16, 640])
        vn = carve(arb, 2048 + 20480, [2048])
        halves = [[(0, 512), (512, 128)], [(640, 384), (1024, 256)]]
        for half in range(2):
            T0 = 640 * half
            chunks = list(range(5 * half, 5 * half + 5))
            for vb in range(4):
                wv = wload(wsrc(wi, 2048 + vb * 512, 512), 8, 512)
                for ci_, c in enumerate(chunks):
                    pb = bank()
                    for k in range(8):
                        P.mm(pb[:, :], hT[:, k, c * 128:(c + 1) * 128], wv[:, k, :], start=(k == 0), stop=False)
                    P.mm(pb[:, :], onesb, bvall[:, vb * 512:(vb + 1) * 512], start=False, stop=True)
                    t = tmp()
                    P.act(t[:, :], pb[:, :], AF.Gelu_apprx_tanh)
                    P.act(sqb[:, vb, :], t[:, :], AF.Square, accum_out=ssq[:, c, vb:vb + 1])
                    P.copy(v_tm[:, ci_, vb * 512:(vb + 1) * 512], t[:, :])
            for ci_, c in enumerate(chunks):
                P.reduce(rst[:, 0:1], ssq[:, c, :], ALU.add)
                P.act(rst[:, 0:1], rst[:, 0:1], AF.Sqrt, scale=1.0 / 2048, bias=eps_ap)
                P.recip(rst[:, 0:1], rst[:, 0:1])
                P.stt(vn, v_tm[:, ci_, :], rst[:, 0:1], gng, ALU.mult, ALU.mult)
                for dq in range(4):
                    pb = bank()
                    for dd in range(4):
                        d = dq * 4 + dd
                        hh = d // 2
                        P.mm(pb[:, dd * 128:(dd + 1) * 128], vn[:, d * 128:(d + 1) * 128], wsT[:, hh, :],
                             start=True, stop=False)
                        P.mm(pb[:, dd * 128:(dd + 1) * 128], onesb, bsall[:, hh * 128:(hh + 1) * 128],
                             start=False, stop=True)
                    P.copy(svT[:, dq * 4:(dq + 1) * 4, ci_ * 128:(ci_ + 1) * 128],
                           pb[:, :].rearrange("p (a b) -> p a b", a=4), eng='act')
            for ub in range(4):
                wv = wload(wsrc(wi, ub * 512, 512), 8, 512)
                for (t0, N) in halves[half]:
                    for dd in range(4):
                        d = ub * 4 + dd
                        pb = bank()
                        for k in range(8):
                            P.mm(pb[:, :N], wv[:, k, dd * 128:(dd + 1) * 128], hT[:, k, t0:t0 + N],
                                 start=(k == 0), stop=(k == 7))
                        t = tmp()
                        P.act(t[:, :N], pb[:, :N], AF.Gelu_apprx_tanh, bias=buT[:, d:d + 1])
                        P.tt(svT[:, d, t0 - T0:t0 - T0 + N], t[:, :N], svT[:, d, t0 - T0:t0 - T0 + N], ALU.mult)
            for mb in range(4):
                wv = wload(wsrc(wo, mb * 256, 256), 16, 256)
                for mh in range(2):
                    m = mb * 2 + mh
                    for (t0, N) in halves[half]:
                        pb = bank()
                        for kk in range(16):
                            P.mm(pb[:, :N], wv[:, kk, mh * 128:(mh + 1) * 128], svT[:, kk, t0 - T0:t0 - T0 + N],
                                 start=(kk == 0), stop=(kk == 15))
                        resid_add(m, t0, N, pb, 1)

    def mixer_fnet():
        dftc = carve(miscb, 0, [2, 512])
        P.dma(dftc, dftc_d, key="mc0")
        dA = carve(arb, 0, [8, 2, 1024])
        dB = carve(arb, 16384, [2, 2, 256])
        UV = carve(arb, 17408, [10, 512])
        P.dma(dA, dftpA_d, key="mc1")
        P.dma(dB, dftpB_d, key="mc2")
        fb = carve(arf, 0, [8])
        P.dma(fb, f_bT, key="mc3")
        for gq in range(4):
            for c in range(10):
                pb = bank()
                for k in range(2):
                    P.mm(pb[:, :], hT[:, 2 * gq + k, c * 128:(c + 1) * 128], dftc[:, k, :], start=(k == 0), stop=(k == 1))
                P.copy(UV[:, c, :], pb[:, :], eng=('act' if c % 2 else 'dve'))
            for dd in range(2):
                d = 2 * gq + dd
                for lg in range(2):
                    pb = bank()
                    for lt in range(8):
                        P.mm(pb[:, :], UV[:, lt, dd * 128:(dd + 1) * 128], dA[:, lt, 0, lg * 512:(lg + 1) * 512],
                             start=(lt == 0), stop=False)
                        P.mm(pb[:, :], UV[:, lt, 256 + dd * 128:256 + (dd + 1) * 128],
                             dA[:, lt, 1, lg * 512:(lg + 1) * 512], start=False, stop=(lt == 7))
                    P.copy(hT[:, d, lg * 512:(lg + 1) * 512], pb[:, :], eng=('act' if lg else 'dve'))
                pb = bank()
                for lt in range(2):
                    P.mm(pb[:, :256], UV[:, 8 + lt, dd * 128:(dd + 1) * 128], dB[:, lt, 0, :], start=(lt == 0), stop=False)
                    P.mm(pb[:, :256], UV[:, 8 + lt, 256 + dd * 128:256 + (dd + 1) * 128], dB[:, lt, 1, :],
                         start=False, stop=(lt == 1))
                P.copy(hT[:, d, 1024:1280], pb[:, :256], eng='act')
        wo = f_w_out[0]
        for mb in range(2):
            wv = wload(wsrc(wo, mb * 512, 512), 8, 512)
            for mm_ in range(4):
                m = mb * 4 + mm_
                for (t0, N) in TG:
                    ci = cidx(t0)
                    pb = bank()
                    for k in range(8):
                        P.mm(pb[:, :N], wv[:, k, mm_ * 128:(mm_ + 1) * 128], hT[:, k, t0:t0 + N],
                             start=(k == 0), stop=(k == 7))
                    t = tmp()
                    P.ts(t[:, :N], pb[:, :N], fb[:, m:m + 1], ALU.add, gtT[:, 1, m, ci:ci + 1], ALU.mult)
                    P.tt(xT[:, m, t0:t0 + N], xT[:, m, t0:t0 + N], t[:, :N], ALU.add)

    def mixer_attn():
        wq = a_w_qkv[0]
        wo = a_w_o[0]
        QT = carve(arb, 0, [8, NT])
        KT = carve(arb, 10240, [4, 1536])
        Vtm = carve(arb, 16384, [12, 320])
        mask = carve(arb, 20224, [10, 1024])
        ckb = carve(miscb, 0, [2, 256])
        P.memset(QT[64:128, :, :], 0.0)
        P.memset(KT[64:128, :, :], 0.0)
        P.memset(Vtm[:, :, 256:320], 0.0)
        P.dma(mask, amask_d.rearrange("t p q -> p t q"), key="mc0")
        qg = misc[:, 0:64]
        kg = misc[:, 64:128]
        rc = carve(misc, 128, [10, 32])
        rsn = carve(misc, 448, [10, 32])
        P.dma(qg, a_qg, key="mc1")
        P.dma(kg, a_kg, key="mc2")
        P.dma(rc, ropec_d, key="mc3")
        P.dma(rsn, ropes_d, key="mc4")
        P.dma(ckb, ck_d.rearrange("(t p) f -> p t f", p=128), key="mc5", queue='pool')
        P.dma(Vtm[:, 0:2, 0:256], cv_d.rearrange("(t p) f -> p t f", p=128), key="mc6", queue='pool')
        for t in range(2):
            for kh in range(4):
                P.transpose(ptb[0:64, kh * 128:(kh + 1) * 128], ckb[:, t, kh * 64:(kh + 1) * 64], identb)
            P.copy(KT[0:64, :, t * 128:(t + 1) * 128], ptb[0:64, 0:512].rearrange("p (a b) -> p a b", a=4))

        ms_ = [carve(arf, 0, [8]), carve(arf, 8, [8])]
        qr_ = [carve(miscb, 0, [512]), carve(miscb, 512, [512])]
        pt_ = [carve(miscb, 1024, [512]), carve(miscb, 1536, [512]), carve(miscb, 2048, [512])]

        def proj_unit(wv, iskv, qgain, c, u):
            p = u % 2
            H = 4 if iskv else 8
            W = H * 64
            pb = bank()
            for k in range(8):
                P.mm(pb[:, :], hT[:, k, c * 128:(c + 1) * 128], wv[:, k, :], start=(k == 0), stop=(k == 7))
            yield
            pbv = pb[:, 0:W]
            t1 = tmps[2 * p]
            t2 = tmps[2 * p + 1]
            ms = ms_[p]
            qr = qr_[p]
            iob = io[c % 2]
            t3 = iob[:, 0:256] if iskv else [rs, io[0]][p][:, :W]
            P.act(t1[:, :W], pbv, AF.Square)
            P.reduce(ms[:, 0:H], t1[:, :W].rearrange("p (h d) -> p h d", d=64), ALU.add)
            P.act(ms[:, 0:H], ms[:, 0:H], AF.Sqrt, scale=1.0 / 64, bias=eps_ap)
            P.recip(ms[:, 0:H], ms[:, 0:H])
            t2v = t2[:, :W].rearrange("p (h d) -> p h d", d=64)
            P.tt(t2v, pbv.rearrange("p (h d) -> p h d", d=64), bc(ms[:, 0:H].unsqueeze(2), [128, H, 64]), ALU.mult)
            t3v = t3.rearrange("p (h d) -> p h d", d=64)
            P.tt(t3v, t2v, bc(qgain.unsqueeze(1), [128, H, 64]), ALU.mult)
            if iskv:
                P.copy(iob[:, 256:512], pb[:, 256:512], eng='act')
                P.copy(Vtm[:, 2 + c, 0:256], pb[:, 256:512], eng='act')
            yield
            x5 = t3.rearrange("p (h a b f) -> p h a b f", h=H, a=2, b=2)
            o5 = qr[:, 0:W].rearrange("p (h a b f) -> p h a b f", h=H, a=2, b=2)
            x1 = x5[:, :, :, 0, :]
            x2 = x5[:, :, :, 1, :]
            cosb = bc(rc[:, c, :].rearrange("p (a f) -> p a f", a=2).unsqueeze(1), [128, H, 2, 16])
            sinb = bc(rsn[:, c, :].rearrange("p (a f) -> p a f", a=2).unsqueeze(1), [128, H, 2, 16])
            Wh = W // 2
            tav = t1[:, :Wh].rearrange("p (h a f) -> p h a f", h=H, a=2)
            tbv = t2[:, :Wh].rearrange("p (h a f) -> p h a f", h=H, a=2)
            P.tt(tav, x1, cosb, ALU.mult)
            P.tt(tbv, x2, sinb, ALU.mult)
            P.tt(o5[:, :, :, 0, :], tav, tbv, ALU.subtract)
            P.tt(tav, x1, sinb, ALU.mult)
            P.tt(tbv, x2, cosb, ALU.mult)
            P.tt(o5[:, :, :, 1, :], tav, tbv, ALU.add)
            if iskv:
                P.dma(newk[c * 128:(c + 1) * 128, :], iob[:, 0:256], key="ok%d" % (c % 2))
                P.dma(newv[c * 128:(c + 1) * 128, :], iob[:, 256:512], key="ov%d" % (c % 2))
            yield
            for h in range(H):
                P.transpose(ptb[0:64, h * 128:(h + 1) * 128], qr[:, h * 64:(h + 1) * 64], identb)
            yield
            if iskv:
                P.copy(KT[0:64, :, 256 + c * 128:256 + (c + 1) * 128],
                       ptb[0:64, 0:512].rearrange("p (a b) -> p a b", a=4), eng='act')
            else:
                P.copy(QT[0:64, :, c * 128:(c + 1) * 128], ptb[0:64, :].rearrange("p (a b) -> p a b", a=8), eng='act')

        mark('a kv')
        wv = wload(wsrc(wq, 1024, 512), 8, 512)
        run_pipelined([proj_unit(wv, True, kg, c, c) for c in range(10)])
        for qb in range(2):
            mark('a q%d proj' % qb)
            wv = wload(wsrc(wq, qb * 512, 512), 8, 512)
            run_pipelined([proj_unit(wv, False, qg, c, c) for c in range(10)])
            mark('a q%d core' % qb)

            def att_unit(hl, kvh, q0, N, si, s, ntl, usemask, po, psm, ctr):
                ps_ = psb[ctr % 3]
                pt = pt_[ctr % 3]
                P.mm(ps_[:, :N], KT[:, kvh, s * 128:(s + 1) * 128], QT[:, hl, q0:q0 + N],
                     start=True, stop=(not usemask))
                if usemask:
                    P.mm(ps_[:, :N], identb, mask[:, s, q0:q0 + N], start=False, stop=True)
                yield
                P.act(pt[:, :N], ps_[:, :N], AF.Exp, scale=0.125)
                yield
                P.mm(po[:, :N], Vtm[:, s, kvh * 64:kvh * 64 + 128], pt[:, :N], start=(si == 0), stop=(si == ntl - 1))
                P.mm(psm[:, :N], onesb, pt[:, :N], start=(si == 0), stop=(si == ntl - 1))
                if si == ntl - 1:
                    yield
                    t = tmp()
                    P.recip(t[0:64, :N], psm[0:64, :N])
                    P.tt(QT[0:64, hl, q0:q0 + N], po[0:64, :N], t[0:64, :N], ALU.mult)

            units = []
            it = 0
            ctr = 0
            for hl in range(8):
                kvh = qb * 2 + hl // 4
                jobs = [(0, 512, list(range(10)), True), (512, 512, list(range(10)), True), (1024, 256, [10, 11], False)]
                for (q0, N, tiles, usemask) in jobs:
                    po = psb[3 + 2 * (it % 2)]
                    psm = psb[4 + 2 * (it % 2)]
                    it += 1
                    for si, s_ in enumerate(tiles):
                        units.append(att_unit(hl, kvh, q0, N, si, s_, len(tiles), usemask, po, psm, ctr))
                        ctr += 1
            run_pipelined(units)
            mark('a q%d out' % qb)
            for mb in range(2):
                src = wo[qb * 512:(qb + 1) * 512, mb * 512:(mb + 1) * 512].rearrange("(h p) n -> p h n", p=64)
                wv2_ = wload(src, 8, 512, parts=64)
                wv2 = slots[(S.si - 1) % NSLOT][:, 0:8 * 512].rearrange("p (k n) -> p k n", n=512)
                for mm_ in range(4):
                    m = mb * 4 + mm_
                    for (t0, N) in TG:
                        pb = bank(0, 3)
                        for hl in range(8):
                            P.mm(pb[:, :N], wv2[:, hl, mm_ * 128:(mm_ + 1) * 128], QT[:, hl, t0:t0 + N],
                                 start=(hl == 0), stop=(hl == 7))
                        resid_add(m, t0, N, pb, 1)

    def mixer_mamba():
        wi = m_w_in[0]
        wo = m_w_out[0]
        hp = hT[:, :, :].rearrange("p k (s w) -> p k s w", w=260)

        def hch(k, c):
            o = 2 + (c % 2) * 128
            return hp[:, k, c // 2, o:o + 128]

        P.memset(hp[:, :, :, 0:2], 0.0)
        P.memset(hp[:, :, :, 258:260], 0.0)
        P.ts(hp[:, :, 1:4, 1:2], hp[:, :, 0:3, 257:258], flag_ap, ALU.mult)
        P.ts(hp[:, :, 0:3, 258:259], hp[:, :, 1:4, 2:3], flag_ap, ALU.mult)
        import os as _os
        SUB = int(_os.environ.get('SUB', '9'))
        SKEW = 1
        if SUB < 1:
            return
        convw = carve(misc, 0, [24, 3])
        convb = carve(misc, 72, [24])
        dtb = carve(misc, 96, [64])
        aneg = carve(misc, 160, [64])
        ddb = carve(misc, 224, [32])
        mngt = carve(misc, 256, [512])
        ssy = carve(misc, 768, [8])
        stg = [carve(misc, 832, [512]), carve(misc, 1344, [512])]
        P.dma(convw, m_convw, key="mc0")
        P.dma(convb, m_convb, key="mc1")
        P.dma(dtb, m_dtb, key="mc2")
        P.dma(aneg, m_alog, key="mc3")
        P.dma(ddb, m_dd, key="mc4")
        P.act(aneg, aneg, AF.Exp)
        P.ts(aneg, aneg, -1.0, ALU.mult)
        if SUB < 2:
            return
        dt = carve(arf, 0, [10, 64])
        ncum = carve(arf, 640, [10, 64])
        ecum = carve(arf, 1280, [10, 64])
        dtt = carve(arf, 1920, [10, 64])
        cdec = carve(arf, 2560, [10, 64])
        stt_ = carve(arf, 3840, [2, 512])
        z_tm = carve(arb, 0, [10, 512])
        x_tm = carve(arb, 5120, [10, 512])
        B_tm = carve(arb, 10240, [10, 128])
        BT = carve(arb, 11520, [NT])
        CT = carve(arb, 12800, [NT])
        hin = carve(arb, 14080, [2, 10, 512])
        yT = carve(arb, 24320, [4, NT])
        CBm_ = [carve(arb, 29440, [2, 128]), carve(arb, 29696, [2, 128])]
        yb = carve(arb, 29952, [512])
        ncp_ = [carve(arb, 30464, [2, 8]), carve(arb, 30480, [2, 8])]
        dsp = carve(arb, 30496, [2, 64])
        sqbf = sqb[:, :, :].rearrange("p a b -> p (a b)")
        xw = carve(sqbf, 0, [512])
        D2n_ = [carve(sqbf, 0, [2, 8, 128]), carve(sqbf, 2048, [2, 8, 128])]
        xdt_ = [carve(miscb, i * 512, [512]) for i in range(3)]
        EM_ = [carve(miscb, 1536 + i * 512, [4, 128]) for i in range(3)]
        mark('m dt')
        wv = wload(wsrc(wi, 5120, 64), 8, 64)
        for c in range(10):
            pb = bank()
            for k in range(8):
                P.mm(pb[:, 0:64], hch(k, c), wv[:, k, :], start=(k == 0), stop=(k == 7))
            t = tmp()
            P.tt(t[:, 0:64], pb[:, 0:64], dtb, ALU.add)
            P.act(t[:, 0:64], t[:, 0:64], AF.Exp)
            P.act(dt[:, c, :], t[:, 0:64], AF.Ln, bias=one_ap)
            if SUB < 3:
                continue
            td = tmp()
            P.tt(td[:, 0:64], dt[:, c, :], aneg, ALU.mult)
            P.copy(dsp[:, 0, :], td[:, 0:64])
            P.tt(td[:, 64:128], td[:, 0:64], dsp[:, 0, :], ALU.subtract)
            P.copy(dsp[:, 1, :], td[:, 64:128])
            if SUB < 4:
                continue
            pc = bank()
            for pi in range(2):
                P.mm(pc[:, 0:32], trifb, dsp[:, pi, 0:32], start=(pi == 0), stop=(pi == 1))
            for pi in range(2):
                P.mm(pc[:, 32:64], tribb, dsp[:, pi, 32:64], start=(pi == 0), stop=(pi == 1))
            for pi in range(2):
                P.mm(pc[:, 64:128], onesb, dsp[:, pi, :], start=(pi == 0), stop=(pi == 1))
            if SUB < 5:
                continue
            SUBX = _os.environ.get('SUBX', 'ta')
            if 't' in SUBX:
                P.ts(ncum[:, c, :], pc[:, 0:64], -1.0, ALU.mult)
            if 'a' in SUBX:
                P.act(ecum[:, c, :], pc[:, 0:64], AF.Exp)
            if SUB < 6:
                continue
            t2 = tmp()
            P.tt(t2[:, 0:64], pc[:, 64:128], ncum[:, c, :], ALU.add)
            P.act(t2[:, 0:64], t2[:, 0:64], AF.Exp)
            P.tt(dtt[:, c, :], t2[:, 0:64], dt[:, c, :], ALU.mult)
            P.act(cdec[:, c, :], pc[:, 64:128], AF.Exp)
        if mstop < 1:
            return
        for g in range(4):
            P.dma(stt_, ssm0[:, :, g * 512:(g + 1) * 512].rearrange("d n f -> n d f"), key="mc5")
            P.dma(mngt, m_ng[:, g * 512:(g + 1) * 512], key="mc6")
            mark('m g%d z' % g)
            wv = wload(wsrc(wi, g * 512, 512), 8, 512)
            for c in range(10):
                pb = bank()
                for k in range(8):
                    P.mm(pb[:, :], hch(k, c), wv[:, k, :], start=(k == 0), stop=(k == 7))
                P.act(z_tm[:, c, :], pb[:, :], AF.Silu)
            mark('m g%d conv' % g)
            wvx = wload(wsrc(wi, 2048 + g * 512, 512), 8, 512)
            s_, key = wslot()
            wvbc = s_[:, 0:8 * 256].rearrange("p (k n) -> p k n", n=256)
            P.dma(wvbc[:, :, 0:128], wsrc(wi, 4096 + g * 128, 128), key=key, queue='pool')
            P.dma(wvbc[:, :, 128:256], wsrc(wi, 4608 + g * 128, 128), key=key, queue='pool', join=True)
            xcs = [carve(sqbf, 512, [256]), carve(sqbf, 768, [256]), carve(sqbf, 1024, [256])]

            def conv_unit(q, s, ctr):
                cc = (g * 4 + q) if q < 4 else (16 + g if q == 4 else 20 + g)
                pb = bank()
                for k in range(8):
                    P.mm(pb[:, 0:260], (wvx[:, k, q * 128:(q + 1) * 128] if q < 4 else wvbc[:, k, (q - 4) * 128:(q - 3) * 128]),
                         hp[:, k, s, :], start=(k == 0), stop=(k == 7))
                yield
                t1 = tmp()
                P.act(t1[:, 0:256], pb[:, 2:258], AF.Identity, scale=convw[:, cc, 1:2], bias=convb[:, cc:cc + 1])
                yield
                P.stt(t1[:, 0:256], pb[:, 1:257], convw[:, cc, 0:1], t1[:, 0:256], ALU.mult, ALU.add)
                P.stt(t1[:, 0:256], pb[:, 3:259], convw[:, cc, 2:3], t1[:, 0:256], ALU.mult, ALU.add)
                yield
                r = ctr % 3
                if q < 4:
                    xc_ = xcs[r]
                    P.act(xc_, t1[:, 0:256], AF.Silu)
                    yield
                    for j in range(2):
                        P.transpose(ptb[:, r * 256 + j * 128:r * 256 + (j + 1) * 128], xc_[:, j * 128:(j + 1) * 128], identb)
                    yield
                    P.copy(x_tm[:, 2 * s:2 * s + 2, q * 128:(q + 1) * 128],
                           ptb[:, r * 256:(r + 1) * 256].rearrange("p (a b) -> p a b", a=2))
                elif q == 4:
                    P.act(BT[:, s * 256:(s + 1) * 256], t1[:, 0:256], AF.Silu)
                    yield
                    for j in range(2):
                        P.transpose(ptb[:, r * 256 + j * 128:r * 256 + (j + 1) * 128],
                                    BT[:, s * 256 + j * 128:s * 256 + (j + 1) * 128], identb)
                    yield
                    P.copy(B_tm[:, 2 * s:2 * s + 2, :], ptb[:, r * 256:(r + 1) * 256].rearrange("p (a b) -> p a b", a=2))
                else:
                    P.act(CT[:, s * 256:(s + 1) * 256], t1[:, 0:256], AF.Silu)

            cu = []
            for q in range(6):
                for s in range(5):
                    cu.append(conv_unit(q, s, len(cu)))
            S.tring = tmps + [rs, io[0], io[1]]
            run_pipelined(cu, skew=SKEW)
            S.tring = tmps
            if mstop < 2:
                continue
            mark('m g%d phA' % g)
            for d in range(2):
                order = (list(range(8)) + [8, 9]) if d == 0 else ([7, 6, 5, 4, 3, 2, 1, 0] + [9, 8])
                sv = stt_[:, d, :]
                sv3 = sv.rearrange("p (h q) -> p h q", q=64)
                for c in order:
                    if (d == 0 and c == 8) or (d == 1 and c == 9):
                        P.memset(sv, 0.0)
                    elif (d == 0 and c in (2, 4, 6)) or (d == 1 and c in (5, 3, 1)):
                        P.ts(sv, sv, flag_ap, ALU.mult)
                    P.copy(hin[:, d, c, :], sv, eng='act')
                    hc0 = d * 32 + g * 8
                    P.tt(xw.rearrange("p (h q) -> p h q", q=64), x_tm[:, c, :].rearrange("p (h q) -> p h q", q=64),
                         bc(dtt[:, c, hc0:hc0 + 8].unsqueeze(2), [128, 8, 64]), ALU.mult)
                    pS = bank()
                    P.mm(pS[:, :], B_tm[:, c, :], xw, start=True, stop=True)
                    P.tt(sv3, sv3, bc(cdec[:, c, hc0:hc0 + 8].unsqueeze(2), [128, 8, 64]), ALU.mult)
                    P.tt(sv, sv, pS[:, :], ALU.add)
                    if (d == 0 and c % 2 == 1) or (d == 1 and c % 2 == 0):
                        sg = stg[S.ti % 2]
                        S.ti += 1
                        P.copy(sg, sv, eng='act')
                        P.dma(ssm_out[c // 2, d, :, g * 512:(g + 1) * 512], sg, key="os%d" % (S.ti % 2))
            if mstop < 3:
                continue
            mark('m g%d phB' % g)
            rring = [psb[0], psb[1], psb[6]]
            v3 = lambda a: a.rearrange("p (h q) -> p h q", q=64)

            def head_unit(c):
                pcb = rring[S.rr % 3]
                S.rr += 1
                P.mm(pcb[:, 0:128], BT[:, c * 128:(c + 1) * 128], CT[:, c * 128:(c + 1) * 128], start=True, stop=True)
                yield
                CBm = CBm_[c % 2]
                P.copy(CBm[:, 0, :], pcb[:, 0:128])

            def quarter_unit(c, d, hb):
                hc0 = d * 32 + g * 8
                cd = c * 2 + d
                ncp = ncp_[cd % 2]
                D2n = D2n_[cd % 2]
                xdt = xdt_[cd % 3]
                CBm = CBm_[c % 2]
                if hb == 0:
                    tn = tmp()
                    P.copy(ncp[:, 0, :], ncum[:, c, hc0:hc0 + 8])
                    P.tt(tn[:, 0:8], ncum[:, c, hc0:hc0 + 8], ncp[:, 0, :], ALU.subtract)
                    P.copy(ncp[:, 1, :], tn[:, 0:8])
                yield
                if hb == 0:
                    for pi in range(2):
                        P.tt(D2n[:, pi], bc(identb.unsqueeze(1), [128, 8, 128]),
                             bc(ncp[:, pi, :].unsqueeze(2), [128, 8, 128]), ALU.mult)
                yield
                pr = rring[S.rr % 3]
                S.rr += 1
                prv = pr[:, :].rearrange("p (h i) -> p h i", h=4)
                for pi in range(2):
                    P.mm(prv, negonesb, D2n[:, pi, hb * 4:(hb + 1) * 4, :], start=(pi == 0), stop=False)
                for pi in range(2):
                    P.mm(prv, identb, bc(ncp[:, pi, hb * 4:(hb + 1) * 4].unsqueeze(2), [128, 4, 128]),
                         start=False, stop=False)
                P.mm(prv, identb, bc((nmfb if d == 0 else nmbb).unsqueeze(1), [128, 4, 128]), start=False, stop=True)
                yield
                EM = EM_[S.em % 3]
                S.em += 1
                P.act(EM.rearrange("p h i -> p (h i)"), pr[:, :], AF.Exp)
                if hb == 0:
                    P.tt(v3(xdt), v3(x_tm[:, c, :]), bc(dt[:, c, hc0:hc0 + 8].unsqueeze(2), [128, 8, 64]), ALU.mult)
                yield
                P.tt(EM, EM, bc(CBm[:, 0, :].unsqueeze(1), [128, 4, 128]), ALU.mult)
                yield
                py = psb[2 + d]
                for hh in range(4):
                    hl = hb * 4 + hh
                    P.mm(py[:, hl * 64:(hl + 1) * 64], EM[:, hh, :], xdt[:, hl * 64:(hl + 1) * 64], start=True, stop=True)

            def tail_unit(c):
                po0 = psb[4]
                po1 = psb[5]
                P.mm(po0[:, :], CT[:, c * 128:(c + 1) * 128], hin[:, 0, c, :], start=True, stop=True)
                P.mm(po1[:, :], CT[:, c * 128:(c + 1) * 128], hin[:, 1, c, :], start=True, stop=True)
                yield
                y1 = [rs, io[0]][c % 2]
                y2 = io[1]
                P.tt(v3(y1[:, :]), v3(po0[:, :]), bc(ecum[:, c, g * 8:g * 8 + 8].unsqueeze(2), [128, 8, 64]), ALU.mult)
                P.tt(v3(y2[:, :]), v3(po1[:, :]), bc(ecum[:, c, 32 + g * 8:32 + g * 8 + 8].unsqueeze(2), [128, 8, 64]), ALU.mult)
                P.tt(y1[:, :], y1[:, :], y2[:, :], ALU.add)
                yield
                P.tt(v3(y2[:, :]), v3(x_tm[:, c, :]), bc(ddb[:, g * 8:g * 8 + 8].unsqueeze(2), [128, 8, 64]), ALU.mult)
                P.tt(y1[:, :], y1[:, :], y2[:, :], ALU.add)
                for _ in range(5):
                    yield
                P.tt(y1[:, :], y1[:, :], psb[2][:, :], ALU.add)
                P.tt(y1[:, :], y1[:, :], psb[3][:, :], ALU.add)
                P.tt(y1[:, :], y1[:, :], z_tm[:, c, :], ALU.mult)
                yield
                P.act(y2[:, :], y1[:, :], AF.Square, accum_out=ssy[:, 0:1])
                P.act(ssy[:, 0:1], ssy[:, 0:1], AF.Sqrt, scale=1.0 / 512, bias=eps_ap)
                yield
                P.recip(ssy[:, 0:1], ssy[:, 0:1])
                P.stt(yb, y1[:, :], ssy[:, 0:1], mngt, ALU.mult, ALU.mult)
                yield
                for q in range(4):
                    P.transpose(ptb[:, q * 128:(q + 1) * 128], yb[:, q * 128:(q + 1) * 128], identb)
                yield
                P.copy(yT[:, :, c * 128:(c + 1) * 128], ptb[:, 0:512].rearrange("p (a b) -> p a b", a=4), eng='act')

            bu = []
            for c in range(10):
                bu.append(head_unit(c))
                for d in range(2):
                    for hb in range(2):
                        bu.append(quarter_unit(c, d, hb))
                bu.append(tail_unit(c))
            run_pipelined(bu, skew=1)
            if mstop < 4:
                continue
            mark('m g%d out' % g)
            wv = wload(wsrc(wo, 0, 1024, r0=g * 512, rows=512), 4, 1024)
            for m in range(8):
                for (t0, N) in TG:
                    pb = bank()
                    for kk in range(4):
                        P.mm(pb[:, :N], wv[:, kk, m * 128:(m + 1) * 128], yT[:, kk, t0:t0 + N],
                             start=(kk == 0), stop=(kk == 3))
                    resid_add(m, t0, N, pb, 1)

    mix_fns = [mixer_mamba, mixer_gmlp, mixer_fnet, mixer_attn]
    hp_dst = lambda k, t0, N: hT[:, :, :].rearrange("p k (s w) -> p k s w", w=260)[:, k, t0 // 256, 2:258]
    SEG = [(s * 256, 256) for s in range(5)]

    for li in range(nlayers):
        mark('L%d adaln' % li)
        if lite:
            if li == 0:
                for _ in xload_gen():
                    pass
            P.memset(modT[:], 0.25)
            P.memset(scT[:], 1.0)
            P.memset(gtT[:], 0.5)
        else:
            if li == 0:
                xg = xload_gen()
                for _ in adaln_gen(0):
                    next(xg, None)
                for _ in xg:
                    pass
            adaln_finish(li)
        mark('L%d ffn1' % li)
        if ffn:
            ffn_half(li, 0, 0, ss_next=bool(mixers[li % 4]))
        mark('L%d mixer' % li)
        kind = li % 4
        if mixers[kind]:
            if kind == 0:
                rmsnorm_mod(1, SEG, hp_dst)
            else:
                rmsnorm_mod(1, TG, hplain)
            mix_fns[kind]()
        mark('L%d ffn2' % li)
        nxt = (li + 1 < nlayers) and not lite
        if ffn:
            if nxt:
                S.nbank = 6
            ffn_half(li, 1, 2, adagen=(adaln_gen(li + 1) if nxt else None), ss_next=(li + 1 < nlayers))
            S.nbank = 7
        elif nxt:
            for _ in adaln_gen(li + 1):
                pass
    mark('output')
    for c in range(10):
        for kq in range(2):
            iob = io[kq]
            pb = bank()
            for kk in range(4):
                k = kq * 4 + kk
                P.transpose(pb[:, kk * 128:(kk + 1) * 128], xT[:, k, c * 128:(c + 1) * 128], identf)
            P.copy(iob[:, :], pb[:, :], eng=('act' if kq else 'dve'))
            P.dma(yout[c * 128:(c + 1) * 128, kq * 512:(kq + 1) * 512], iob[:], key="io%d" % kq)

    P.emit(st)
    st.close()
    S.P = P
    return nc, S


def _consts():
    i = np.arange(128)
    ident = np.eye(128, dtype=np.float32)
    tri_f = (i[:, None] <= i[None, :]).astype(np.float32)
    tri_b = (i[:, None] >= i[None, :]).astype(np.float32)
    ones = np.ones((128, 128), np.float32)
    cst_f = np.stack([ident, tri_f, tri_b, ones], axis=1)
    nm_f = np.where(i[:, None] > i[None, :], NEG, 0.0).astype(np.float32)
    nm_b = np.where(i[:, None] < i[None, :], NEG, 0.0).astype(np.float32)
    cst_b = np.stack([ident, tri_f, tri_b, ones, -ones, nm_f, nm_b], axis=1).astype(NPBF)
    n = np.arange(256)
    ang = 2 * np.pi * np.outer(n, n) / 256.0
    cs = np.concatenate([np.cos(ang), np.sin(ang)], axis=1) / 16.0
    dftc = cs.reshape(2, 128, 512).transpose(1, 0, 2).astype(NPBF)

    def pos(L):
        m = np.arange(L)
        a = 2 * np.pi * np.outer(m, m) / float(L)
        return np.cos(a) / np.sqrt(L), -np.sin(a) / np.sqrt(L)
    c1024, s1024 = pos(1024)
    c256, s256 = pos(256)
    fullA = np.stack([c1024, s1024], axis=1)
    blkc = np.zeros((1024, 1024)); blks = np.zeros((1024, 1024))
    for s in range(4):
        blkc[s * 256:(s + 1) * 256, s * 256:(s + 1) * 256] = c256
        blks[s * 256:(s + 1) * 256, s * 256:(s + 1) * 256] = s256
    blkA = np.stack([blkc, blks], axis=1)
    toA = lambda a: a.reshape(8, 128, 2, 1024).transpose(1, 0, 2, 3).astype(NPBF)
    dftpA_s = toA(fullA)
    dftpA_p = toA(blkA)
    dftpB = np.stack([c256, s256], axis=1).reshape(2, 128, 2, 256).transpose(1, 0, 2, 3).astype(NPBF)
    am_s = np.zeros((10, 128, 1024), np.float32)
    am_p = np.full((1280, 1024), NEG, np.float32)
    for s in range(4):
        am_p[256 + s * 256:256 + (s + 1) * 256, s * 256:(s + 1) * 256] = 0.0
    am_p = am_p.reshape(10, 128, 1024)
    t = np.arange(1024)
    inv = 10000.0 ** (-np.arange(16, dtype=np.float32) / 16.0)
    angr = (t // 64).astype(np.float32)[:, None] * inv[None, :]
    angc = (t % 64).astype(np.float32)[:, None] * inv[None, :]
    cosT = np.ones((1280, 32), np.float32)
    sinT = np.zeros((1280, 32), np.float32)
    cosT[:1024, :16] = np.cos(angr); cosT[:1024, 16:] = np.cos(angc)
    sinT[:1024, :16] = np.sin(angr); sinT[:1024, 16:] = np.sin(angc)
    tot = lambda a: np.ascontiguousarray(a.reshape(10, 128, 32).transpose(1, 0, 2))
    rope_s = (tot(cosT), tot(sinT))
    rope_p = (tot(np.ones((1280, 32), np.float32)), tot(np.zeros((1280, 32), np.float32)))
    return dict(cst_f=np.ascontiguousarray(cst_f), cst_b=np.ascontiguousarray(cst_b), dftc=np.ascontiguousarray(dftc),
                dftpA_s=np.ascontiguousarray(dftpA_s), dftpA_p=np.ascontiguousarray(dftpA_p),
                dftpB=np.ascontiguousarray(dftpB), am_s=am_s.astype(NPBF), am_p=am_p.astype(NPBF),
                rope_s=rope_s, rope_p=rope_p)


def _pp(v, chunks):
    return np.ascontiguousarray(np.asarray(v, np.float32).reshape(chunks, 128).T)


def _rep(v):
    v = np.asarray(v, np.float32).reshape(1, -1)
    return np.ascontiguousarray(np.broadcast_to(v, (128, v.shape[1])))


def make_in_maps(inp):
    C = _consts()
    f = lambda a: np.ascontiguousarray(np.asarray(a, np.float32))
    shared = {
        "dftpB": C["dftpB"], "dftc": C["dftc"], "cst_f": C["cst_f"], "cst_b": C["cst_b"],
        "ln_gT": np.ascontiguousarray(f(inp["ln_g"]).reshape(4, 3, 8, 128).transpose(3, 0, 1, 2)),
        "ada_bT": np.ascontiguousarray(f(inp["ada_b"]).reshape(4, 72, 128).transpose(2, 0, 1)),
        "ada_w": f(inp["ada_w"]),
        "ff1_w_in": f(inp["ff1_w_in"]), "ff2_w_in": f(inp["ff2_w_in"]),
        "ff1_w_out": f(inp["ff1_w_out"]), "ff2_w_out": f(inp["ff2_w_out"]),
        "m_w_in": f(inp["m_w_in"]),
        "m_convw": np.ascontiguousarray(f(inp["m_conv_w"])[0].reshape(3, 24, 128).transpose(2, 1, 0)),
        "m_convb": _pp(f(inp["m_conv_b"])[0], 24),
        "m_dtb": _rep(f(inp["m_dt_bias"])[0].reshape(-1)),
        "m_alog": _rep(f(inp["m_a_log"])[0].reshape(-1)),
        "m_dd": _rep(f(inp["m_d"])[0]),
        "m_ng": _rep(f(inp["m_norm_g"])[0]),
        "m_w_out": f(inp["m_w_out"]),
        "g_w_in": f(inp["g_w_in"]),
        "g_buT": _pp(f(inp["g_b_in"])[0, :2048], 16),
        "g_bv": np.ascontiguousarray(f(inp["g_b_in"])[0:1, 2048:]),
        "g_ng": _rep(f(inp["g_norm_g"])[0]),
        "g_w_s": f(inp["g_w_s"]),
        "g_b_s": np.ascontiguousarray(f(inp["g_b_s"]).reshape(1, 1024)),
        "g_w_out": f(inp["g_w_out"]),
        "f_w_out": f(inp["f_w_out"]),
        "f_bT": _pp(f(inp["f_b_out"])[0], 8),
        "a_w_qkv": f(inp["a_w_qkv"]),
        "a_qg": _rep(f(inp["a_q_norm"])[0]),
        "a_kg": _rep(f(inp["a_k_norm"])[0]),
        "a_w_o": f(inp["a_w_o"]),
    }
    xp = f(inp["x_prompt"]); xs = f(inp["x_sample"])
    c = f(inp["c"]); cctx = f(inp["c_ctx"])
    maps = []
    for core in range(8):
        m = dict(shared)
        if core < 6:
            seqs = list(range(5 * core, 5 * core + 5))
            xin = xp[seqs].reshape(NT, D)
            cond = np.stack([cctx, cctx], 0)
            m["flag"] = np.zeros((128, 1), np.float32)
            m["ssm0"] = np.zeros((2, 128, 2048), np.float32)
            m["ck"] = np.zeros((256, 256), np.float32)
            m["cv"] = np.zeros((256, 256), np.float32)
            m["amask"] = C["am_p"]; m["ropec"], m["ropes"] = C["rope_p"]
            m["dftpA"] = C["dftpA_p"]
        else:
            b = core - 6
            xin = np.concatenate([xs[b], xp[30 + b]], 0)
            cond = np.stack([c[b], cctx], 0)
            m["flag"] = np.ones((128, 1), np.float32)
            s0 = f(inp["state_ssm"])[b, 0]
            m["ssm0"] = np.ascontiguousarray(s0.reshape(2, 2048, 128).transpose(0, 2, 1))
            m["ck"] = np.ascontiguousarray(f(inp["cache_k"])[b, 0].reshape(256, 256))
            m["cv"] = np.ascontiguousarray(f(inp["cache_v"])[b, 0].reshape(256, 256))
            m["amask"] = C["am_s"]; m["ropec"], m["ropes"] = C["rope_s"]
            m["dftpA"] = C["dftpA_s"]
        m["xin"] = np.ascontiguousarray(xin)
        m["condT"] = np.ascontiguousarray(cond.reshape(2, 8, 128).transpose(2, 1, 0))
        maps.append(m)
    return maps


def assemble(results):
    yp = np.zeros((32, 256, D), np.float32)
    ys = np.zeros((2, 1024, D), np.float32)
    ssm = np.zeros((32, 1, 2, 32, 64, 128), np.float32)
    nk = np.zeros((32, 1, 256, 4, 64), np.float32)
    nv = np.zeros((32, 1, 256, 4, 64), np.float32)
    for core in range(8):
        r = results[core]
        y = np.asarray(r["yout"]).reshape(5, 256, D)
        so = np.asarray(r["ssm_out"]).reshape(5, 2, 128, 32, 64).transpose(0, 1, 3, 4, 2)
        k = np.asarray(r["newk"]).reshape(5, 256, 4, 64)
        v = np.asarray(r["newv"]).reshape(5, 256, 4, 64)
        if core < 6:
            sl = slice(5 * core, 5 * core + 5)
            yp[sl] = y; ssm[sl, 0] = so; nk[sl, 0] = k; nv[sl, 0] = v
        else:
            b = core - 6
            ys[b] = y[:4].reshape(1024, D)
            yp[30 + b] = y[4]; ssm[30 + b, 0] = so[4]; nk[30 + b, 0] = k[4]; nv[30 + b, 0] = v[4]
    return yp, ys, ssm, nk, nv


_CACHE = {}


def kernel(**inputs):
    if "nc" not in _CACHE:
        _CACHE["nc"] = build()[0]
    nc = _CACHE["nc"]
    maps = make_in_maps(inputs)
    res = run_bass_kernel_spmd(nc, maps, core_ids=list(range(8)))
    return assemble(res.results)
```
